# Optimizing a Trainium2 kernel written in Bass

```python
import math
import jax, jax.numpy as jnp
from jax import lax
import numpy as np

D_MODEL = 1024
BATCH = 2
SEQ = 16384
DEPTH = 2
DEC_BATCH = 32
DEC_SEQ = 2048
PAST_LEN = 128

PLE_DIM = 256
N_EVEN = (DEPTH + 1) // 2
N_ODD = DEPTH // 2
A_WIDTH = 1024
A_HEADS = 16
A_HEAD_DIM = A_WIDTH // A_HEADS
A_CONV = 4
A_CONV_PAD = (2, 1)
LRU_C = 8.0
B_WIDTH = 1024
B_CONV = 31
B_CONV_PAD = (B_CONV // 2, B_CONV // 2)
C_WIDTH = 2048
C_HEADS = 16
C_HEAD_DIM = C_WIDTH // C_HEADS
CHUNK = 128
EVEN_IN = 2 * A_WIDTH + 3 * B_WIDTH
EVEN_OUT = A_WIDTH + B_WIDTH
ODD_IN = 3 * C_WIDTH
DN_ALPHA = (2 * DEPTH) ** 0.25
DN_BETA = (8 * DEPTH) ** -0.25
LN_EPS = 1e-5

kernel_name = "hybrid_rglru_conformer_gmlp_encoder"


def layer_norm(x, g, b):
    xf = x.astype(jnp.float32)
    mu = xf.mean(-1, keepdims=True)
    var = jnp.square(xf - mu).mean(-1, keepdims=True)
    y = (xf - mu) * lax.rsqrt(var + LN_EPS)
    return (y * g.astype(jnp.float32) + b.astype(jnp.float32)).astype(x.dtype)


def depthwise_conv(x, w, b, pad):
    y = lax.conv_general_dilated(
        x, w[:, None, :].astype(x.dtype), window_strides=(1,), padding=[pad],
        dimension_numbers=('NWC', 'WIO', 'NWC'), feature_group_count=x.shape[-1])
    return y + b.astype(x.dtype)


def _lin_combine(e1, e2):
    a1, b1 = e1
    a2, b2 = e2
    return a1 * a2, a2 * b1 + b2


def rg_lru_dir(x, w_r, b_r, w_i, b_i, lam, reverse):
    bn, s, _ = x.shape
    xh = x.reshape(bn, s, A_HEADS, A_HEAD_DIM)
    r = jax.nn.sigmoid(jnp.einsum('bshi,hij->bshj', xh, w_r.astype(jnp.float32)) + b_r.astype(jnp.float32))
    i = jax.nn.sigmoid(jnp.einsum('bshi,hij->bshj', xh, w_i.astype(jnp.float32)) + b_i.astype(jnp.float32))
    r = r.reshape(bn, s, A_WIDTH)
    i = i.reshape(bn, s, A_WIDTH)
    log_a = -LRU_C * r * jax.nn.softplus(-lam.astype(jnp.float32))
    a = jnp.exp(log_a)
    u = jnp.sqrt(-jnp.expm1(2.0 * log_a)) * (i * x)
    _, h = lax.associative_scan(_lin_combine, (a, u), axis=1, reverse=reverse)
    return h


def even_mixer(x, w_in, conv_a_w, conv_a_b, lru_r_w, lru_r_b, lru_i_w, lru_i_b, lru_lam,
               conv_b_w, conv_b_b, lnb_g, lnb_b, w_out):
    z = x @ w_in
    xa, ga, ba, bb, gb = jnp.split(
        z, [A_WIDTH, 2 * A_WIDTH, 2 * A_WIDTH + B_WIDTH, 2 * A_WIDTH + 2 * B_WIDTH], axis=-1)
    xa = depthwise_conv(xa, conv_a_w, conv_a_b, A_CONV_PAD).astype(jnp.float32)
    ha = (rg_lru_dir(xa, lru_r_w[0], lru_r_b[0], lru_i_w[0], lru_i_b[0], lru_lam[0], False)
          + rg_lru_dir(xa, lru_r_w[1], lru_r_b[1], lru_i_w[1], lru_i_b[1], lru_lam[1], True))
    ya = ha.astype(x.dtype) * jax.nn.silu(ga)
    hb = ba * jax.nn.sigmoid(bb)
    hb = depthwise_conv(hb, conv_b_w, conv_b_b, B_CONV_PAD)
    yb = jax.nn.silu(layer_norm(hb, lnb_g, lnb_b)) * jax.nn.silu(gb)
    return jnp.concatenate([ya, yb], axis=-1) @ w_out


def odd_mixer(x, w_in, lnv_g, lnv_b, sgu_w, sgu_b, w_out):
    bn, s, _ = x.shape
    z = x @ w_in
    u, v, g = jnp.split(z, 3, axis=-1)
    v = layer_norm(v, lnv_g, lnv_b).reshape(bn, s // CHUNK, CHUNK, C_HEADS, C_HEAD_DIM)
    sv = jnp.einsum('bnphc,hqp->bnqhc', v, sgu_w) + sgu_b.T[:, :, None]
    y = u * sv.reshape(bn, s, C_WIDTH) * jax.nn.silu(g)
    return y @ w_out


def trunk(x, p, even_w_in, conv_a_w, conv_a_b, lru_r_w, lru_r_b, lru_i_w, lru_i_b, lru_lam,
          conv_b_w, conv_b_b, lnb_g, lnb_b, even_w_out, odd_w_in, lnv_g, lnv_b, sgu_w, sgu_b,
          odd_w_out, ln_g, ln_b, ple_w, ple_gate_w):
    for li in range(DEPTH):
        j = li // 2
        if li % 2 == 0:
            f = even_mixer(x, even_w_in[j], conv_a_w[j], conv_a_b[j], lru_r_w[j], lru_r_b[j],
                           lru_i_w[j], lru_i_b[j], lru_lam[j], conv_b_w[j], conv_b_b[j],
                           lnb_g[j], lnb_b[j], even_w_out[j])
        else:
            f = odd_mixer(x, odd_w_in[j], lnv_g[j], lnv_b[j], sgu_w[j], sgu_b[j], odd_w_out[j])
        x = layer_norm(DN_ALPHA * x + f, ln_g[li], ln_b[li])
        x = x + (p[li] @ ple_w[li]) * jax.nn.sigmoid(x @ ple_gate_w[li])
    return x


def setup_inputs(seed: int = 0) -> dict:
    key = jax.random.key(seed)
    ks = jax.random.split(key, 32)
    nrm = lambda k, shape, scale: jax.random.normal(k, shape, jnp.float32) * scale
    a_c = jax.random.uniform(ks[10], (N_EVEN, 2, A_WIDTH), jnp.float32, 0.9, 0.999)
    s_base = a_c ** (1.0 / LRU_C)
    lru_lam = jnp.log(s_base) - jnp.log1p(-s_base)
    return {
        "x_prompt": nrm(ks[0], (BATCH, SEQ, D_MODEL), 1.0),
        "x_sample": nrm(ks[1], (DEC_BATCH, DEC_SEQ, D_MODEL), 1.0),
        "p_prompt": nrm(ks[2], (DEPTH, BATCH, SEQ, PLE_DIM), 1.0),
        "p_sample": nrm(ks[3], (DEPTH, DEC_BATCH, DEC_SEQ, PLE_DIM), 1.0),
        "even_w_in": nrm(ks[4], (N_EVEN, D_MODEL, EVEN_IN), D_MODEL ** -0.5),
        "conv_a_w": nrm(ks[5], (N_EVEN, A_CONV, A_WIDTH), A_CONV ** -0.5),
        "conv_a_b": nrm(ks[6], (N_EVEN, A_WIDTH), 0.01),
        "lru_r_w": nrm(ks[7], (N_EVEN, 2, A_HEADS, A_HEAD_DIM, A_HEAD_DIM), A_HEAD_DIM ** -0.5),
        "lru_r_b": nrm(ks[8], (N_EVEN, 2, A_HEADS, A_HEAD_DIM), 0.01),
        "lru_i_w": nrm(ks[9], (N_EVEN, 2, A_HEADS, A_HEAD_DIM, A_HEAD_DIM), A_HEAD_DIM ** -0.5),
        "lru_i_b": nrm(ks[11], (N_EVEN, 2, A_HEADS, A_HEAD_DIM), 0.01),
        "lru_lam": lru_lam,
        "conv_b_w": nrm(ks[12], (N_EVEN, B_CONV, B_WIDTH), B_CONV ** -0.5),
        "conv_b_b": nrm(ks[13], (N_EVEN, B_WIDTH), 0.01),
        "lnb_g": 1.0 + nrm(ks[14], (N_EVEN, B_WIDTH), 0.01),
        "lnb_b": nrm(ks[15], (N_EVEN, B_WIDTH), 0.01),
        "even_w_out": nrm(ks[16], (N_EVEN, EVEN_OUT, D_MODEL), EVEN_OUT ** -0.5 * DN_BETA),
        "odd_w_in": nrm(ks[17], (N_ODD, D_MODEL, ODD_IN), D_MODEL ** -0.5),
        "lnv_g": 1.0 + nrm(ks[18], (N_ODD, C_WIDTH), 0.01),
        "lnv_b": nrm(ks[19], (N_ODD, C_WIDTH), 0.01),
        "sgu_w": nrm(ks[20], (N_ODD, C_HEADS, CHUNK, CHUNK), CHUNK ** -0.5),
        "sgu_b": 1.0 + nrm(ks[21], (N_ODD, C_HEADS, CHUNK), 0.01),
        "odd_w_out": nrm(ks[22], (N_ODD, C_WIDTH, D_MODEL), C_WIDTH ** -0.5 * DN_BETA),
        "ln_g": 1.0 + nrm(ks[23], (DEPTH, D_MODEL), 0.01),
        "ln_b": nrm(ks[24], (DEPTH, D_MODEL), 0.01),
        "ple_w": nrm(ks[25], (DEPTH, PLE_DIM, D_MODEL), PLE_DIM ** -0.5),
        "ple_gate_w": nrm(ks[26], (DEPTH, D_MODEL, D_MODEL), D_MODEL ** -0.5),
    }


def reference(x_prompt, x_sample, p_prompt, p_sample, even_w_in, conv_a_w, conv_a_b, lru_r_w,
              lru_r_b, lru_i_w, lru_i_b, lru_lam, conv_b_w, conv_b_b, lnb_g, lnb_b, even_w_out,
              odd_w_in, lnv_g, lnv_b, sgu_w, sgu_b, odd_w_out, ln_g, ln_b, ple_w, ple_gate_w):
    y_prompt = trunk(x_prompt, p_prompt, even_w_in, conv_a_w, conv_a_b, lru_r_w, lru_r_b,
                     lru_i_w, lru_i_b, lru_lam, conv_b_w, conv_b_b, lnb_g, lnb_b, even_w_out,
                     odd_w_in, lnv_g, lnv_b, sgu_w, sgu_b, odd_w_out, ln_g, ln_b, ple_w, ple_gate_w)
    y_sample = trunk(x_sample, p_sample, even_w_in, conv_a_w, conv_a_b, lru_r_w, lru_r_b,
                     lru_i_w, lru_i_b, lru_lam, conv_b_w, conv_b_b, lnb_g, lnb_b, even_w_out,
                     odd_w_in, lnv_g, lnv_b, sgu_w, sgu_b, odd_w_out, ln_g, ln_b, ple_w, ple_gate_w)
    return (y_prompt, y_sample)
```

```python
import math
import numpy as np
import concourse.bass as bass
import concourse.mybir as mybir
from concourse.bass_utils import run_bass_kernel_spmd

F32 = mybir.dt.float32
BF16 = mybir.dt.bfloat16
AF = mybir.ActivationFunctionType
ALU = mybir.AluOpType

DM = 1024
NCH = 8
ALPHA = 4 ** 0.25
LN_EPS = 1e-5
EPOCH = 20000


class Sched:
    ENG = ("pe", "act", "dve", "pool", "sp")

    def __init__(self, nc):
        self.nc = nc
        self.prog = {e: [] for e in self.ENG}
        self.count = {e: 0 for e in self.ENG}
        self.esem = {e: [] for e in self.ENG}
        self.waited = {e: {} for e in self.ENG}
        self.lastw = {}
        self.readers = {}
        self.dsem = {}
        self.nsem = 0
        self.alias = {}

    def _x(self, names):
        out = list(names)
        for n in names:
            out.extend(self.alias.get(n, ()))
        return out

    def _newsem(self, name):
        self.nsem += 1
        return self.nc.alloc_semaphore(name=name)

    def _eng_token(self, e):
        k = self.count[e]
        ep = k // EPOCH
        while len(self.esem[e]) <= ep:
            self.esem[e].append(self._newsem(f"s_{e}_{len(self.esem[e])}"))
        self.count[e] = k + 1
        return (self.esem[e][ep], k % EPOCH + 1, e)

    def _emit_waits(self, e, toks):
        w = self.waited[e]
        need = {}
        for t in toks:
            if t is None:
                continue
            sem, val, te = t
            if te == "pe" and e == "pe":
                continue
            key = id(sem)
            if w.get(key, 0) >= val:
                continue
            if key not in need or need[key][1] < val:
                need[key] = (sem, val)
        for key, (sem, val) in need.items():
            w[key] = val
            self.prog[e].append(lambda eng, sem=sem, val=val: eng.wait_ge(sem, val))

    def _deps(self, reads, writes):
        reads = self._x(reads); writes = self._x(writes)
        toks = []
        for b in reads:
            toks.append(self.lastw.get(b))
        for b in writes:
            toks.append(self.lastw.get(b))
            toks.extend(self.readers.get(b, ()))
        return toks

    def _commit(self, tok, reads, writes):
        reads = self._x(reads); writes = self._x(writes)
        for b in reads:
            self.readers.setdefault(b, []).append(tok)
        for b in writes:
            self.lastw[b] = tok
            self.readers[b] = []

    def op(self, e, fn, reads=(), writes=()):
        self._emit_waits(e, self._deps(reads, writes))
        tok = self._eng_token(e)
        sem = tok[0]
        self.prog[e].append(lambda eng, fn=fn, sem=sem: fn(eng).then_inc(sem, 1))
        self._commit(tok, reads, writes)

    def group(self, e, fns, reads=(), writes=()):
        self._emit_waits(e, self._deps(reads, writes))
        tok = self._eng_token(e)
        sem = tok[0]
        for fn in fns[:-1]:
            self.prog[e].append(lambda eng, fn=fn: fn(eng))
        self.prog[e].append(lambda eng, fn=fns[-1], sem=sem: fn(eng).then_inc(sem, 1))
        self._commit(tok, reads, writes)

    def dma(self, q, out, in_, reads=(), writes=(), key=None):
        self._emit_waits(q, self._deps(reads, writes))
        if key is None:
            key = (writes[0] if writes else reads[0])
        if key not in self.dsem:
            self.dsem[key] = [self._newsem(f"d_{self.nsem}"), 0]
        ent = self.dsem[key]
        if ent[1] + 16 > 30000:
            ent[0] = self._newsem(f"d_{self.nsem}")
            ent[1] = 0
        ent[1] += 16
        tok = (ent[0], ent[1], None)
        sem = ent[0]
        self.prog[q].append(lambda eng, out=out, in_=in_, sem=sem: eng.dma_start(out=out, in_=in_).then_inc(sem, 16))
        self._commit(tok, reads, writes)

    def barrier(self):
        toks = []
        for e in ("pe", "act", "dve", "pool", "sp"):
            k = self.count[e]
            if k > 0:
                toks.append((self.esem[e][(k - 1) // EPOCH], (k - 1) % EPOCH + 1, "x" + e))
        for ent in self.dsem.values():
            toks.append((ent[0], ent[1], None))
        for e in self.ENG:
            self._emit_waits(e, toks)

    def finish(self, e="sp"):
        self._emit_waits(e, list(self.lastw.values()))

    def emit(self):
        with self.nc.Block() as block:
            @block.tensor
            def _(eng):
                for f in self.prog["pe"]:
                    f(eng)

            @block.scalar
            def _(eng):
                for f in self.prog["act"]:
                    f(eng)

            @block.vector
            def _(eng):
                for f in self.prog["dve"]:
                    f(eng)

            @block.gpsimd
            def _(eng):
                for f in self.prog["pool"]:
                    f(eng)

            @block.sync
            def _(eng):
                for f in self.prog["sp"]:
                    f(eng)


def vec_layout():
    off = {}
    n = 0
    for name, w in [("caw", 32), ("cab", 8), ("gb", 32), ("lam", 16), ("cbw", 248), ("cbb", 8), ("lnbg", 8),
                    ("lnbb", 8), ("lng", 16), ("lnb", 16), ("lnvg", 16)]:
        off[name] = n
        n += w
    return off, n


def build(cfg):
    SL = cfg["SL"]; TT = SL // 512; SE = SL + 30
    NSS = cfg["n_samp"]; SPS = cfg["segs_per_samp"]; NPA = cfg["n_prompt_all"]; NPO = cfg["n_prompt_own"]
    NSAMP = NSS * SPS
    NMAIN = NSAMP + NPO
    NP1 = NSAMP + NPA
    VO, NV = vec_layout()
    NRING = cfg.get("nring", 4)

    nc = bass.Bass("TRN2", target_bir_lowering=False)
    dt = lambda name, shape, kind="ExternalInput": nc.dram_tensor(name, shape, F32, kind=kind).ap()
    xe = dt("xe", [NP1 + NPO, DM, SE])
    pe_ = dt("pe", [NMAIN, 2, 256, SL])
    msk_d = dt("msk", [128, 2 * NPO * NPA])
    wA = dt("wA", [8, 128, 5120])
    wO0 = dt("wO0", [4, 128, 4096])
    wPP = dt("wPP", [8, 128, 2560])
    wUG = dt("wUG", [8, 128, 4096])
    wV = dt("wV", [4, 128, 4096])
    wO1 = dt("wO1", [4, 128, 4096])
    sguT_d = dt("sguT", [128, 16, 128])
    gw_d = dt("gw", [4, 16, 64, 64])
    vec_d = dt("vec", [128, NV])
    lnvb_d = dt("lnvb", [1, 2048])
    sgub_d = dt("sgub", [1, 2048])
    ident_d = dt("ident", [128, 128])
    y_d = dt("y", [NMAIN, DM, SL], kind="ExternalOutput")

    S = Sched(nc)
    sb = lambda name, shape, d=F32: nc.alloc_sbuf_tensor("sb_" + name, shape, d)
    ring = sb("ring", [128, NRING, 4096], BF16)
    cda = sb("cda", [128, 32, 128], BF16)
    gm = sb("gm", [128, 32, 128], BF16)
    WT = sb("WT", [128, 16, 128], BF16)
    Csb = sb("Csb", [128, 16, 128])
    ident_f = sb("ident_f", [128, 128])
    ident_b = sb("ident_b", [128, 128], BF16)
    onesM = sb("onesM", [128, 128], BF16)
    ones1 = sb("ones1", [1, 128])
    vec = sb("vec", [128, NV])
    cf = sb("cf", [128, 80])
    msk = sb("msk_sb", [128, 2 * NPO * NPA])
    EP = sb("EP", [128, 4, NP1 * 8])
    FB = sb("FB", [128, 2, NP1 * 8])
    ini = sb("ini", [128, 2, NMAIN * 8])
    racc = sb("racc", [128, 2 * TT])
    rsum = sb("rsum", [128, 2])
    xbf = sb("xbf", [128, 8, SE], BF16)
    xtail = sb("xtail", [128, 8, 4], BF16)
    ya = sb("ya", [128, 8, SL], BF16)
    ug = sb("ug", [128, 16, 512], BF16)
    vn = sb("vn", [128, 4, 2048], BF16)
    dg = sb("dg", [128, 2, 31 * 128], BF16)
    hbx = sb("hbx", [128, 2, 544], BF16)
    sgt = sb("sgt", [128, 2, 544], BF16)
    hbc = sb("hbc", [128, 8, 512], BF16)
    yb = sb("yb", [128, 8, 512], BF16)
    tq = sb("tq", [128, 2, 512], BF16)
    tb = sb("tb", [128, 2, 512], BF16)
    tf = sb("tf", [128, 2, 512])
    st_mu = sb("st_mu", [128, 512])
    st_rs = sb("st_rs", [128, 512])
    st_t = sb("st_t", [128, 512])
    xr = sb("xr", [128, 2, 512])
    s_ = sb("s_", [128, 8, 512])
    pbf = sb("pbf", [128, 2, 512], BF16)
    bst = sb("bst", [128, 4, 24])
    mv = sb("mv", [128, 4, 2])
    nrm = sb("nrm", [128, 4, 2])
    xlnb = hbc
    x1b = yb
    svt = tf
    assert 2 * SL <= 2048 and 4 * TT <= 8
    xa_b = [vn[:, 0, 0:SL + 8], dg[:, 0, 0:SL + 8]]
    xc_b = [vn[:, 1, 0:SL], vn[:, 1, SL:2 * SL]]
    mb_b = [vn[:, 2, 0:SL], vn[:, 2, SL:2 * SL]]
    sga_b = [vn[:, 3, 0:SL], vn[:, 3, SL:2 * SL]]
    thr = ug[:, 0:2 * TT, :].rearrange("p (d t) n -> p d (t n)", d=2)
    thi = ug[:, 2 * TT:4 * TT, :].rearrange("p (d t) n -> p d (t n)", d=2)
    Fh = s_[:, 0:2 * TT, :].rearrange("p (d t) n -> p d (t n)", d=2)
    Fa_b = [s_[:, 2 * TT + d * TT:2 * TT + (d + 1) * TT, :].rearrange("p t n -> p (t n)") for d in range(2)]
    Fu_b = [xr[:, 0:TT, :].rearrange("p t n -> p (t n)"), tf[:, 0:TT, :].rearrange("p t n -> p (t n)")]
    ps = nc.alloc_psum_tensor("ps", [128, 8, 512], F32)

    for p_ in range(2):
        for t_ in range(TT + 1):
            S.alias[("xa", p_, t_)] = [("vn", 0, t_)] if p_ == 0 else [("dg", 0)]
        for t_ in range(TT):
            S.alias[("xc", p_, t_)] = [("vn", 1, p_ * TT + t_)]
            S.alias[("sga", p_, t_)] = [("vn", 3, p_ * TT + t_)]
    for d_ in range(2):
        S.alias[("mb", d_)] = [("vn", 2, d_ * TT + t_) for t_ in range(TT)]
        S.alias[("Fh", d_)] = [("s", d_ * TT + t_) for t_ in range(TT)]
        S.alias[("Fa", d_)] = [("s", 2 * TT + d_ * TT + t_) for t_ in range(TT)]
        S.alias[("Fu", d_)] = [("xr", 0), ("xr", 1)] if d_ == 0 else [("tf", 0), ("tf", 1)]
        for t_ in range(TT):
            S.alias[("thr", d_, t_)] = [("ug", d_ * TT + t_)]
            S.alias[("thi", d_, t_)] = [("ug", 2 * TT + d_ * TT + t_)]
    st = {"ring": 0, "bank": 0, "tg": 0, "wa": [None, None, None, None]}

    def nb():
        b = st["bank"]
        st["bank"] = (b + 1) % 6
        return b

    def nb2():
        b = ((st["bank"] + 1) // 2 * 2) % 6
        st["bank"] = (b + 2) % 6
        return b

    MERGE = (TT == 2 and cfg.get("merge", 0))

    def ps2(b):
        return ps[:, b:b + 2, :].rearrange("p a n -> p (a n)")

    def wp(src, n):
        i = st["ring"]
        st["ring"] = (i + 1) % NRING
        S.dma("pool", ring[:, i, 0:n], src, writes=[("w", i)])
        return ring[:, i, :], ("w", i)

    def mm(out, pairs, reads, writes):
        n = len(pairs)
        fns = [(lambda e, l=l, r=r, i=i: e.matmul(out, lhsT=l, rhs=r, start=(i == 0), stop=(i == n - 1)))
               for i, (l, r) in enumerate(pairs)]
        S.group("pe", fns, reads=reads, writes=writes)

    def act(out, in_, func, reads, writes, scale=1.0, bias=0.0, eng="act"):
        S.op(eng, lambda e: e.activation(out=out, in_=in_, func=func, scale=scale, bias=bias), reads=reads, writes=writes)

    def tt(eng, out, a, b, op, reads, writes):
        S.op(eng, lambda e: e.tensor_tensor(out=out, in0=a, in1=b, op=op), reads=reads, writes=writes)

    def stt(out, a, scalar, b, op0, op1, reads, writes):
        S.op("dve", lambda e: e.scalar_tensor_tensor(out=out, in0=a, scalar=scalar, in1=b, op0=op0, op1=op1),
             reads=reads, writes=writes)

    def ts(eng, out, a, s1, s2, op0, op1, reads, writes):
        if s2 is None:
            S.op(eng, lambda e: e.tensor_scalar(out=out, in0=a, scalar1=s1, scalar2=None, op0=op0), reads=reads, writes=writes)
        else:
            S.op(eng, lambda e: e.tensor_scalar(out=out, in0=a, scalar1=s1, scalar2=s2, op0=op0, op1=op1), reads=reads, writes=writes)

    POOLE = cfg.get("pool_eng", "dve")
    V = lambda name, i: vec[:, VO[name] + i: VO[name] + i + 1]

    S.dma("sp", vec[:], vec_d, writes=["vec"])
    S.dma("sp", ident_f[:], ident_d, writes=["ident_f"])
    S.dma("sp", msk[:], msk_d, writes=["msk"])
    S.dma("sp", s_[:, 4:8, :], sguT_d.rearrange("p (a b) q -> p a (b q)", b=4), writes=["WT32"])
    S.dma("sp", xr[0:1, :, :], sgub_d[:, 0:1024].rearrange("o (a b) -> o a b", b=512), writes=["sgub0"])
    S.dma("sp", tf[0:1, :, :], sgub_d[:, 1024:2048].rearrange("o (a b) -> o a b", b=512), writes=["sgub1"])
    S.dma("sp", s_[:, 0:4, :], lnvb_d.partition_broadcast(128).rearrange("p o (a b) -> p (o a) b", b=512), writes=["lnvb"])
    S.dma("pool", WT[:], sguT_d, writes=["WT"])
    S.dma("pool", ident_b[:], ident_d, writes=["ident_b"])
    S.op("dve", lambda e: e.memset(gm[:], 0.0), writes=["gm"])
    S.op("dve", lambda e: e.memset(onesM[:], 1.0 / 1024), writes=["onesM"])
    S.op("dve", lambda e: e.memset(ones1[:], 1.0), writes=["ones1"])
    for kind in range(4):
        for h in range(16):
            j, hh = h // 2, h % 2
            S.dma("pool", gm[hh * 64:(hh + 1) * 64, kind * 8 + j, hh * 64:(hh + 1) * 64], gw_d[kind, h], reads=[], writes=["gm"], key="gmdma")
    for j in range(8):
        for k in range(4):
            act(cda[:, j * 4 + k, :], ident_f[:], AF.Copy, ["ident_f", "vec"], [("cda", j, k)], scale=V("caw", j * 4 + k))
    lam = vec[:, VO["lam"]:VO["lam"] + 16]
    act(cf[:, 0:16], lam, AF.Exp, ["vec"], ["cf0"], scale=-1.0)
    act(cf[:, 0:16], cf[:, 0:16], AF.Ln, ["cf0"], ["cf0"], bias=1.0)
    ts("dve", cf[:, 16:32], cf[:, 0:16], -4.0, None, ALU.mult, None, ["cf0"], ["cf1"])
    ts("dve", cf[:, 0:16], cf[:, 0:16], -8.0, None, ALU.mult, None, ["cf0", "cf1"], ["cf0"])
    ts("dve", cf[:, 32:64], vec[:, VO["gb"]:VO["gb"] + 32], 0.5, None, ALU.mult, None, ["vec"], ["cf2"])
    ts("dve", cf[:, 64:80], cf[:, 16:32], float(SL), None, ALU.mult, None, ["cf1"], ["cf3"])
    CF = ["cf0", "cf1", "cf2", "cf3"]
    for h in range(16):
        b = nb()
        sgs = (xr if h < 8 else tf)[0:1, (h % 8) // 4, (h % 4) * 128:(h % 4 + 1) * 128]
        mm(ps[:, b, 0:128], [(s_[:, h // 4, (h % 4) * 128:(h % 4 + 1) * 128], s_[:, 4 + h // 4, (h % 4) * 128:(h % 4 + 1) * 128]), (ones1[:], sgs)],
           ["lnvb", "WT32", "ones1", "sgub0", "sgub1"], [("ps", b)])
        act(Csb[:, h, :], ps[:, b, 0:128], AF.Copy, [("ps", b)], ["Csb"])

    S.barrier()

    def load_x(xi, alt=False):
        for kc in range(8):
            if not alt:
                S.dma("pool", xbf[:, kc, :], xe[xi, kc * 128:(kc + 1) * 128, :], writes=["xbf"], key="xbfdma")
            else:
                S.dma("pool", ya[:, kc, :], xe[xi, kc * 128:(kc + 1) * 128, 13:13 + SL], writes=["xalt"], key="xaltdma")
                S.dma("pool", xtail[:, kc, 0:3], xe[xi, kc * 128:(kc + 1) * 128, 13 + SL:16 + SL], writes=["xalt"], key="xaltdma")

    def xsrc(alt, kc, e0, N):
        if not alt:
            return xbf[:, kc, e0:e0 + N]
        if e0 == 13 + SL:
            return xtail[:, kc, 0:N]
        return ya[:, kc, e0 - 13:e0 - 13 + N]

    def ap1(j, alt, main):
        xa = xa_b[j % 2]; xn = "xalt" if alt else "xbf"
        if j + 2 < 8:
            st["wa"][(j + 2) % 4] = wp(wA[j + 2, :, 0:(2048 if main else 1024)], 2048 if main else 1024)
        wxa, nxa = st["wa"][j % 4]
        b2 = nb2() if MERGE else None
        for t in range(TT + 1):
            e0 = 13 + 512 * t
            N = 512 if t < TT else 3
            b = (b2 + t) if (MERGE and t < TT) else nb()
            mm(ps[:, b, 0:N], [(wxa[:, kc * 128:(kc + 1) * 128], xsrc(alt, kc, e0, N)) for kc in range(8)], [nxa, xn], [("ps", b)])
            if MERGE and t < TT:
                if t == TT - 1:
                    act(xa[:, 0:SL], ps2(b2), AF.Copy, [("ps", b2), ("ps", b2 + 1)], [("xa", j % 2, 0), ("xa", j % 2, 1)])
            else:
                act(xa[:, 512 * t:512 * t + N], ps[:, b, 0:N], AF.Copy, [("ps", b)], [("xa", j % 2, t)])

    def ap2(j, main, dirs):
        xa = xa_b[j % 2]; xc = xc_b[j % 2]; sga = sga_b[j % 2]; p = j % 2
        b2 = nb2() if MERGE else None
        for t in range(TT):
            b = (b2 + t) if MERGE else nb()
            mm(ps[:, b, :], [(cda[:, j * 4 + k, :], xa[:, 512 * t + k:512 * t + k + 512]) for k in range(4)],
               [("xa", p, t), ("xa", p, t + 1)] + [("cda", j, k) for k in range(4)], [("ps", b)])
            if not MERGE:
                act(xc[:, 512 * t:512 * (t + 1)], ps[:, b, :], AF.Identity, [("ps", b), "vec"], [("xc", p, t)], bias=V("cab", j))
        if MERGE:
            act(xc, ps2(b2), AF.Identity, [("ps", b2), ("ps", b2 + 1), "vec"], [("xc", p, 0), ("xc", p, 1)], bias=V("cab", j))
        if main:
            wslot, nga = st["wa"][j % 4]
            wga = wslot[:, 1024:2048]
        if MERGE:
            for d in dirs:
                for gi, dst, nm in ((0, thr, "thr"), (1, thi, "thi")):
                    b2 = nb2()
                    for t in range(TT):
                        mm(ps[:, b2 + t, :], [(gm[:, (d * 2 + gi) * 8 + j, :], xc[:, 512 * t:512 * (t + 1)])], ["gm", ("xc", p, t)], [("ps", b2 + t)])
                    bia = cf[:, 32 + (d * 2 + gi) * 8 + j:32 + (d * 2 + gi) * 8 + j + 1]
                    wr = [(nm, d, t) for t in range(TT)]
                    if gi == 0 and not main:
                        S.op("act", lambda e, dst=dst, d=d, b2=b2, bia=bia: e.activation(
                            out=dst[:, d, :], in_=ps2(b2), func=AF.Tanh, scale=0.5, bias=bia,
                            accum_out=racc[:, d * TT:d * TT + 1]), reads=[("ps", b2), ("ps", b2 + 1)] + CF, writes=wr + [("racc", d, 0)])
                    else:
                        act(dst[:, d, :], ps2(b2), AF.Tanh, [("ps", b2), ("ps", b2 + 1)] + CF, wr, scale=0.5, bias=bia)
            if main:
                b2 = nb2()
                for t in range(TT):
                    mm(ps[:, b2 + t, :], [(wga[:, kc * 128:(kc + 1) * 128], xbf[:, kc, 15 + 512 * t:15 + 512 * (t + 1)]) for kc in range(8)],
                       [nga, "xbf"], [("ps", b2 + t)])
                act(sga, ps2(b2), AF.Silu, [("ps", b2), ("ps", b2 + 1)], [("sga", p, t) for t in range(TT)])
            return
        for t in range(TT):
            sl = slice(512 * t, 512 * (t + 1))
            for d in dirs:
                for gi, dst, nm in ((0, thr, "thr"), (1, thi, "thi")):
                    b = nb()
                    mm(ps[:, b, :], [(gm[:, (d * 2 + gi) * 8 + j, :], xc[:, sl])], ["gm", ("xc", p, t)], [("ps", b)])
                    bia = cf[:, 32 + (d * 2 + gi) * 8 + j:32 + (d * 2 + gi) * 8 + j + 1]
                    if gi == 0 and not main:
                        S.op("act", lambda e, dst=dst, d=d, sl=sl, b=b, bia=bia, t=t: e.activation(
                            out=dst[:, d, sl], in_=ps[:, b, :], func=AF.Tanh, scale=0.5, bias=bia,
                            accum_out=racc[:, d * TT + t:d * TT + t + 1]), reads=[("ps", b)] + CF, writes=[(nm, d, t), ("racc", d, t)])
                    else:
                        act(dst[:, d, sl], ps[:, b, :], AF.Tanh, [("ps", b)] + CF, [(nm, d, t)], scale=0.5, bias=bia)
            if main:
                b = nb()
                mm(ps[:, b, :], [(wga[:, kc * 128:(kc + 1) * 128], xbf[:, kc, 15 + 512 * t:15 + 512 * (t + 1)]) for kc in range(8)],
                   [nga, "xbf"], [("ps", b)])
                act(sga[:, sl], ps[:, b, :], AF.Silu, [("ps", b)], [("sga", p, t)])

    def ap3(j, main, idx, dirs):
        xc = xc_b[j % 2]; sga = sga_b[j % 2]; p = j % 2
        for d in dirs:
            chalf = cf[:, 16 + d * 8 + j:16 + d * 8 + j + 1]
            act(Fa_b[d], thr[:, d, :], AF.Exp, [("thr", d, t) for t in range(TT)] + CF, [("Fa", d)], scale=chalf, bias=chalf)
            tt(POOLE, Fu_b[d], Fa_b[d], Fa_b[d], ALU.mult, [("Fa", d)], [("Fu", d)])
        for d in dirs:
            act(Fu_b[d], Fu_b[d], AF.Ln, [("Fu", d)], [("Fu", d)], scale=-1.0, bias=1.0 + 1e-6)
            act(mb_b[d], Fu_b[d], AF.Exp, [("Fu", d)], [("mb", d)], scale=0.5, bias=math.log(0.5))
            if not main:
                c0 = idx * 8 + j
                if TT == 2 and not MERGE:
                    tt("dve", rsum[:, d:d + 1], racc[:, d * TT:d * TT + 1], racc[:, d * TT + 1:d * TT + 2], ALU.add,
                       [("racc", d, 0), ("racc", d, 1)], [("rsum", d)])
                    rs_ap, rs_n = rsum[:, d:d + 1], ("rsum", d)
                else:
                    rs_ap, rs_n = racc[:, d * TT:d * TT + 1], ("racc", d, 0)
                chalf = cf[:, 16 + d * 8 + j:16 + d * 8 + j + 1]
                act(EP[:, 2 + d, c0:c0 + 1], rs_ap, AF.Exp, [rs_n] + CF, [("EP", 2 + d)], scale=chalf, bias=cf[:, 64 + d * 8 + j:64 + d * 8 + j + 1])
        for d in dirs:
            Fa = Fa_b[d]; Fu = Fu_b[d]
            stt(Fu, thi[:, d, :], 1.0, xc, ALU.add, ALU.mult, [("thi", d, t) for t in range(TT)] + [("xc", p, t) for t in range(TT)] + [("mb", d)], [("Fu", d)])
            tt(POOLE, Fu, Fu, mb_b[d], ALU.mult, [("Fu", d), ("mb", d)], [("Fu", d)])
            if main:
                init = ini[:, d, idx * 8 + j:idx * 8 + j + 1]
                rinit = ["ini"]
            else:
                init = 0.0
                rinit = []
            if d == 0:
                S.op("dve", lambda e, init=init, Fa=Fa, Fu=Fu: e.tensor_tensor_scan(out=Fh[:, 0, :], data0=Fa, data1=Fu, initial=init,
                                                                                   op0=ALU.mult, op1=ALU.add), reads=[("Fa", d), ("Fu", d)] + rinit, writes=[("Fh", 0)])
            else:
                S.op("dve", lambda e, init=init, Fa=Fa, Fu=Fu: e.tensor_tensor_scan(out=Fh[:, 1, ::-1], data0=Fa[:, ::-1], data1=Fu[:, ::-1], initial=init,
                                                                                   op0=ALU.mult, op1=ALU.add), reads=[("Fa", d), ("Fu", d)] + rinit, writes=[("Fh", 1)])
            if not main:
                col = SL - 1 if d == 0 else 0
                c0 = idx * 8 + j
                S.op("dve", lambda e, d=d, col=col, c0=c0: e.tensor_copy(out=EP[:, d, c0:c0 + 1], in_=Fh[:, d, col:col + 1]),
                     reads=[("Fh", d)], writes=[("EP", d)])
            elif d == 0 and ((idx < NSAMP and idx % 2 == 0) or (NSAMP <= idx < NMAIN - 1)):
                c1 = (idx + 1) * 8 + j
                S.op("dve", lambda e, c1=c1: e.tensor_copy(out=ini[:, 0, c1:c1 + 1], in_=Fh[:, 0, SL - 1:SL]),
                     reads=[("Fh", 0), "ini"], writes=["ini"])
        if main:
            tt(POOLE, Fh[:, 0, :], Fh[:, 0, :], Fh[:, 1, :], ALU.add, [("Fh", 0), ("Fh", 1)], [("Fh", 0)])
            tt(POOLE, ya[:, j, :], Fh[:, 0, :], sga, ALU.mult, [("Fh", 0)] + [("sga", p, t) for t in range(TT)], [("ya", j)])

    def apath_seg(main, idx, alt=False, dirs=(0, 1)):
        for j0 in range(2):
            st["wa"][j0] = wp(wA[j0, :, 0:(2048 if main else 1024)], 2048 if main else 1024)
        ap1(0, alt, main)
        for j in range(8):
            if j + 1 < 8:
                ap1(j + 1, alt, main)
            ap2(j, main, dirs)
            ap3(j, main, idx, dirs)

    def ln_stats_finish():
        act(st_mu[:], ps[:, 6, :], AF.Copy, [("ps", 6)], ["st_mu"])
        act(st_t[:], ps[:, 6, :], AF.Square, [("ps", 6)], ["st_t"])
        tt("dve", st_t[:], ps[:, 7, :], st_t[:], ALU.subtract, [("ps", 7), "st_t"], ["st_t"])
        ts("dve", st_t[:], st_t[:], LN_EPS, None, ALU.add, None, ["st_t"], ["st_t"])
        act(st_t[:], st_t[:], AF.Ln, ["st_t"], ["st_t"])
        act(st_rs[:], st_t[:], AF.Exp, ["st_t"], ["st_rs"], scale=-0.5)

    def stats_acc(src_bf, m, rd):
        i = st["tg"]; st["tg"] ^= 1
        act(tq[:, i, :], src_bf, AF.Square, rd, [("tq", i)])
        mm(ps[:, 6, :], [(onesM[:], src_bf)], rd + ["onesM"], [("ps", 6)]) if False else None
        S.group("pe", [lambda e: e.matmul(ps[:, 6, :], lhsT=onesM[:], rhs=src_bf, start=(m == 0), stop=(m == 7))],
                reads=rd + ["onesM"], writes=[("ps", 6)])
        S.group("pe", [lambda e: e.matmul(ps[:, 7, :], lhsT=onesM[:], rhs=tq[:, i, :], start=(m == 0), stop=(m == 7))],
                reads=[("tq", i), "onesM"], writes=[("ps", 7)])

    def ln_ple(li, s, t0, pull=lambda n: None):
        for m in range(8):
            i = st["tg"]
            act(tb[:, i, :], s_[:, m, :], AF.Copy, [("s", m)], [("tb", i)])
            stats_acc(tb[:, i, :], m, [("tb", i)])
        pull(1)
        ln_stats_finish()
        S.dma("pool", pbf[:], pe_[s, li, :, t0:t0 + 512].rearrange("(a p) n -> p a n", p=128), writes=["pbf"])
        for m in range(8):
            if m in (3, 6):
                pull(1)
            tt("dve", s_[:, m, :], s_[:, m, :], st_mu[:], ALU.subtract, [("s", m), "st_mu"], [("s", m)])
            tt("dve", s_[:, m, :], s_[:, m, :], st_rs[:], ALU.mult, [("s", m), "st_rs"], [("s", m)])
            act(s_[:, m, :], s_[:, m, :], AF.Identity, [("s", m), "vec"], [("s", m)], scale=V("lng", li * 8 + m), bias=V("lnb", li * 8 + m))
            act(xlnb[:, m, :], s_[:, m, :], AF.Copy, [("s", m)], [("hbc", m)])
        for m in range(8):
            if m % 2 == 0:
                wpp, ng = wp(wPP[li * 4 + m // 2], 2560)
            wg = wpp[:, (m % 2) * 1280:(m % 2) * 1280 + 1024]
            wl = wpp[:, (m % 2) * 1280 + 1024:(m % 2 + 1) * 1280]; nl = ng
            b = nb()
            mm(ps[:, b, :], [(wg[:, kc * 128:(kc + 1) * 128], xlnb[:, kc, :]) for kc in range(8)],
               [ng] + [("hbc", kc) for kc in range(8)], [("ps", b)])
            b2 = nb()
            mm(ps[:, b2, :], [(wl[:, kc * 128:(kc + 1) * 128], pbf[:, kc, :]) for kc in range(2)], [nl, "pbf"], [("ps", b2)])
            i = st["tg"]; st["tg"] ^= 1
            act(tf[:, i, :], ps[:, b, :], AF.Tanh, [("ps", b)], [("tf", i)], scale=0.5)
            stt(tf[:, i, :], tf[:, i, :], 1.0, ps[:, b2, :], ALU.add, ALU.mult, [("tf", i), ("ps", b2)], [("tf", i)])
            stt(s_[:, m, :], tf[:, i, :], 0.5, s_[:, m, :], ALU.mult, ALU.add, [("tf", i), ("s", m)], [("s", m)])
            if li == 0:
                act(x1b[:, m, :], s_[:, m, :], AF.Copy, [("s", m)], [("yb", m)])

    def phase_b(s, xi, t, next_xi=None):
        t0 = 512 * t
        hoist_out = (TT == 2 and t == 0 and cfg.get("hoist", 1))
        hoisted_in = (TT == 2 and t == 1 and cfg.get("hoist", 1))

        def conv_steps(tb0, dst, dname):
            def bs1(j):
                i = j % 2
                S.op("dve", lambda e, i=i, j=j: e.tensor_tensor(
                    out=dg[:, i, :].rearrange("p (k c) -> p k c", c=128),
                    in0=ident_b[:].unsqueeze(1).broadcast_to([128, 31, 128]),
                    in1=vec[:, VO["cbw"] + j * 31:VO["cbw"] + (j + 1) * 31].unsqueeze(2).broadcast_to([128, 31, 128]),
                    op=ALU.mult), reads=["ident_b", "vec"], writes=[("dg", i)])
                wab, na = wp(wA[j, :, 2048:4096], 2048)
                wa = wab[:, 0:1024]; wb_ = wab[:, 1024:2048]
                for (c0, N) in ((0, 512), (512, 30)):
                    ba = nb()
                    mm(ps[:, ba, 0:N], [(wa[:, kc * 128:(kc + 1) * 128], xbf[:, kc, tb0 + c0:tb0 + c0 + N]) for kc in range(8)], [na, "xbf"], [("ps", ba)])
                    bb = nb()
                    mm(ps[:, bb, 0:N], [(wb_[:, kc * 128:(kc + 1) * 128], xbf[:, kc, tb0 + c0:tb0 + c0 + N]) for kc in range(8)], [na, "xbf"], [("ps", bb)])
                    act(sgt[:, i, c0:c0 + N], ps[:, bb, 0:N], AF.Tanh, [("ps", bb)], [("sgt", i, c0)], scale=0.5)
                    stt(hbx[:, i, c0:c0 + N], sgt[:, i, c0:c0 + N], 1.0, ps[:, ba, 0:N], ALU.add, ALU.mult,
                        [("sgt", i, c0), ("ps", ba)], [("hbx", i, c0)])

            def bs2(j):
                i = j % 2
                b = nb()
                mm(ps[:, b, :], [(dg[:, i, k * 128:(k + 1) * 128], hbx[:, i, k:k + 512]) for k in range(31)],
                   [("dg", i), ("hbx", i, 0), ("hbx", i, 512)], [("ps", b)])
                act(dst(j), ps[:, b, :], AF.Identity, [("ps", b), "vec"], [(dname, j)], scale=0.5, bias=V("cbb", j))

            steps = [lambda: bs1(0)]
            for j in range(8):
                if j + 1 < 8:
                    steps.append(lambda j=j: (bs1(j + 1), bs2(j)))
                else:
                    steps.append(lambda j=j: bs2(j))
            return steps

        hoisted_in2 = (t == 0 and st.get("b0_hoisted"))
        if t == 0:
            st["b0_hoisted"] = False
        if hoisted_in:
            hsrc = lambda j: ya[:, j, 0:512]
            hname = "ya"
        elif hoisted_in2:
            hsrc = lambda j: yb[:, j, :]
            hname = "yb"
        else:
            hsrc = lambda j: hbc[:, j, :]
            hname = "hbc"
            for stp in conv_steps(t0, hsrc, hname):
                stp()
        for j in range(8):
            i = j % 2
            act(tq[:, i, :], hsrc(j), AF.Square, [(hname, j)], [("tq", i)])
            S.group("pe", [lambda e, j=j: e.matmul(ps[:, 6, :], lhsT=onesM[:], rhs=hsrc(j), start=(j == 0), stop=(j == 7))],
                    reads=[(hname, j), "onesM"], writes=[("ps", 6)])
            S.group("pe", [lambda e, j=j, i=i: e.matmul(ps[:, 7, :], lhsT=onesM[:], rhs=tq[:, i, :], start=(j == 0), stop=(j == 7))],
                    reads=[("tq", i), "onesM"], writes=[("ps", 7)])
        pend = conv_steps(512, lambda j: ya[:, j, 0:512], "ya") if hoist_out else []

        def pull(n):
            for _ in range(n):
                if pend:
                    pend.pop(0)()
        ln_stats_finish()
        for j in range(8):
            i = st["tg"]; st["tg"] ^= 1
            tt("dve", tf[:, i, :], hsrc(j), st_mu[:], ALU.subtract, [(hname, j), "st_mu"], [("tf", i)])
            tt("dve", tf[:, i, :], tf[:, i, :], st_rs[:], ALU.mult, [("tf", i), "st_rs"], [("tf", i)])
            act(tb[:, i, :], tf[:, i, :], AF.Silu, [("tf", i), "vec"], [("tb", i)], scale=V("lnbg", j), bias=V("lnbb", j))
            wg_, ng_ = wp(wA[j, :, 4096:5120], 1024)
            b = nb()
            mm(ps[:, b, :], [(wg_[:, kc * 128:(kc + 1) * 128], xbf[:, kc, 15 + t0:15 + t0 + 512]) for kc in range(8)], [ng_, "xbf"], [("ps", b)])
            act(tq[:, i, :], ps[:, b, :], AF.Silu, [("ps", b)], [("tq", i)])
            tt("dve", yb[:, j, :], tb[:, i, :], tq[:, i, :], ALU.mult, [("tb", i), ("tq", i)], [("yb", j)])
        hoist_next = (next_xi is not None and t == TT - 1 and cfg.get("hoist2", 1))
        if next_xi is not None and t == TT - 1:
            load_x(next_xi)
        if cfg.get("stop") == "b1":
            return
        for m in range(8):
            if m % 2 == 0:
                wo2, no = wp(wO0[m // 2], 4096)
            wo = wo2[:, (m % 2) * 2048:(m % 2 + 1) * 2048]
            b = nb()
            prs = [(wo[:, kc * 128:(kc + 1) * 128], ya[:, kc, t0:t0 + 512]) for kc in range(8)]
            prs += [(wo[:, (8 + kc) * 128:(9 + kc) * 128], yb[:, kc, :]) for kc in range(8)]
            mm(ps[:, b, :], prs, [no] + [("ya", kc) for kc in range(8)] + [("yb", kc) for kc in range(8)], [("ps", b)])
            S.dma("sp", xr[:, m % 2, :], xe[xi, m * 128:(m + 1) * 128, 15 + t0:15 + t0 + 512], writes=[("xr", m % 2)])
            stt(s_[:, m, :], xr[:, m % 2, :], ALPHA, ps[:, b, :], ALU.mult, ALU.add, [("xr", m % 2), ("ps", b)], [("s", m)])
        if cfg.get("stop") == "b2":
            return
        ln_ple(0, s, t0, pull if not hoist_next else (lambda n: None))
        if cfg.get("stop") == "b3":
            return
        for h in range(16):
            if h % 2 == 0:
                wug, nu = wp(wUG[h // 2], 4096)
            wu = wug[:, (h % 2) * 2048:(h % 2) * 2048 + 1024]
            wg2 = wug[:, (h % 2) * 2048 + 1024:(h % 2 + 1) * 2048]; ng2 = nu
            bu = nb()
            mm(ps[:, bu, :], [(wu[:, kc * 128:(kc + 1) * 128], x1b[:, kc, :]) for kc in range(8)], [nu] + [("yb", kc) for kc in range(8)], [("ps", bu)])
            bg = nb()
            mm(ps[:, bg, :], [(wg2[:, kc * 128:(kc + 1) * 128], x1b[:, kc, :]) for kc in range(8)], [ng2] + [("yb", kc) for kc in range(8)], [("ps", bg)])
            i = st["tg"]; st["tg"] ^= 1
            act(tb[:, i, :], ps[:, bg, :], AF.Silu, [("ps", bg)], [("tb", i)])
            tt("dve", ug[:, h, :], ps[:, bu, :], tb[:, i, :], ALU.mult, [("ps", bu), ("tb", i)], [("ug", h)])
        for cg in range(4):
            wv01, nv0 = wp(wV[cg], 4096)
            wv0 = wv01[:, 0:2048]; wv1 = wv01[:, 2048:4096]; nv1 = nv0
            for n in range(4):
                b = nb()
                prs = [(x1b[:, kc, n * 128:(n + 1) * 128], (wv0 if kc < 4 else wv1)[:, (kc % 4) * 512:(kc % 4 + 1) * 512]) for kc in range(8)]
                mm(ps[:, b, :], prs, [nv0, nv1] + [("yb", kc) for kc in range(8)], [("ps", b)])
                act(vn[:, n, cg * 512:(cg + 1) * 512], ps[:, b, :], AF.Copy, [("ps", b)], [("vn", n, cg)])
                S.op("dve", lambda e, n=n, cg=cg: e.bn_stats(out=bst[:, n, cg * 6:(cg + 1) * 6], in_=vn[:, n, cg * 512:(cg + 1) * 512]), reads=[("vn", n, cg)], writes=[("bst", n, cg)])
        if hoist_next:
            pend.extend(conv_steps(0, lambda j: yb[:, j, :], "yb"))
            st["b0_hoisted"] = True
        pull(1)
        for n in range(4):
            if n in (1, 3):
                pull(1)
            S.op("dve", lambda e, n=n: e.bn_aggr(out=mv[:, n, :], in_=bst[:, n, :]), reads=[("bst", n, cg) for cg in range(4)], writes=[("mv", n)])
            ts("dve", nrm[:, n, 0:1], mv[:, n, 1:2], LN_EPS, None, ALU.add, None, [("mv", n)], [("nrm", n)])
            act(nrm[:, n, 0:1], nrm[:, n, 0:1], AF.Ln, [("nrm", n)], [("nrm", n)])
            act(nrm[:, n, 0:1], nrm[:, n, 0:1], AF.Exp, [("nrm", n)], [("nrm", n)], scale=-0.5)
            tt("dve", nrm[:, n, 1:2], mv[:, n, 0:1], nrm[:, n, 0:1], ALU.mult, [("mv", n), ("nrm", n)], [("nrm2", n)])
            ts("dve", nrm[:, n, 1:2], nrm[:, n, 1:2], -1.0, None, ALU.mult, None, [("nrm2", n)], [("nrm2", n)])
            act(vn[:, n, :], vn[:, n, :], AF.Identity, [("vn", n, cg) for cg in range(4)] + [("nrm", n), ("nrm2", n)],
                [("vn", n, cg) for cg in range(4)], scale=nrm[:, n, 0:1], bias=nrm[:, n, 1:2])
        for h in range(16):
            if h in (2, 6, 10, 14):
                pull(1)
            b = nb()
            fns = [(lambda e, n=n, b=b, h=h: e.matmul(ps[:, b, n * 128:(n + 1) * 128], lhsT=vn[:, n, h * 128:(h + 1) * 128], rhs=WT[:, h, :],
                                                     start=True, stop=True)) for n in range(4)]
            S.group("pe", fns, reads=["WT"] + [("vn", n, h // 4) for n in range(4)], writes=[("ps", b)])
            i = st["tg"]; st["tg"] ^= 1
            stt(svt[:, i, :].rearrange("p (n q) -> p n q", q=128), ps[:, b, :].rearrange("p (n q) -> p n q", q=128), V("lnvg", h),
                Csb[:, h, :].unsqueeze(1).broadcast_to([128, 4, 128]), ALU.mult, ALU.add, [("ps", b), "Csb", "vec"], [("tf", i)])
            tt(cfg.get("sgu_eng", "dve"), ug[:, h, :], ug[:, h, :], svt[:, i, :], ALU.mult, [("ug", h), ("tf", i)], [("ug", h)])
        if cfg.get("stop") == "b4":
            return
        for m in range(8):
            if m % 2 == 0:
                wo2, no = wp(wO1[m // 2], 4096)
            wo = wo2[:, (m % 2) * 2048:(m % 2 + 1) * 2048]
            b = nb()
            mm(ps[:, b, :], [(wo[:, kc * 128:(kc + 1) * 128], ug[:, kc, :]) for kc in range(16)], [no] + [("ug", h) for h in range(16)], [("ps", b)])
            stt(s_[:, m, :], s_[:, m, :], ALPHA, ps[:, b, :], ALU.mult, ALU.add, [("s", m), ("ps", b)], [("s", m)])
        ln_ple(1, s, t0, pull)
        pull(99)
        for m in range(8):
            S.dma("sp", y_d[s, m * 128:(m + 1) * 128, t0:t0 + 512], s_[:, m, :], reads=[("s", m)], writes=[("yout", m)])

    assert SPS == 2
    p1list = [k for k in range(NP1) if not (k < NSAMP and k % 2 == 0)]
    if cfg.get("stop") == "prologue":
        p1list = []
    if p1list:
        load_x(p1list[0], alt=False)
    for ii, k in enumerate(p1list):
        if ii + 1 < len(p1list):
            load_x(p1list[ii + 1], alt=((ii + 1) % 2 == 1))
        if k < NSAMP:
            dirs = (1,)
        elif k == NSAMP and NPA > 1:
            dirs = (0,)
        elif k - NSAMP >= NPA - NPO and NPA > 1:
            dirs = (1,)
        else:
            dirs = (0, 1)
        apath_seg(False, k, alt=(ii % 2 == 1), dirs=dirs)
    seqs = [list(range(NSAMP, NSAMP + NPA))]
    EPn = [("EP", i) for i in range(4)]
    S.op("dve", lambda e: e.memset(FB[:], 0.0), writes=["FB"])
    for sq in seqs:
        for d in range(2):
            if len(sq) > 1:
                order = sq[:NPA - NPO] if d == 0 else sq[::-1][:-1]
            else:
                order = sq
            prev = None
            for k in order:
                o = FB[:, d, k * 8:(k + 1) * 8]
                if prev is None:
                    S.op("dve", lambda e, o=o, d=d, k=k: e.tensor_copy(out=o, in_=EP[:, d, k * 8:(k + 1) * 8]), reads=EPn, writes=["FB"])
                else:
                    pv = FB[:, d, prev * 8:(prev + 1) * 8]
                    tt("dve", o, EP[:, 2 + d, k * 8:(k + 1) * 8], pv, ALU.mult, EPn + ["FB"], ["FB"])
                    tt("dve", o, o, EP[:, d, k * 8:(k + 1) * 8], ALU.add, EPn + ["FB"], ["FB"])
                prev = k
    S.op("dve", lambda e: e.memset(ini[:], 0.0), writes=["ini"])
    for i in range(NSS):
        for r in range(SPS):
            k = i * SPS + r
            if r < SPS - 1:
                S.op("dve", lambda e, k=k: e.tensor_copy(out=ini[:, 1, k * 8:(k + 1) * 8], in_=EP[:, 1, (k + 1) * 8:(k + 2) * 8]), reads=EPn, writes=["ini"])
    for o in range(NPO):
        s = NSAMP + o
        for d in range(2):
            if d == 0 and o > 0:
                continue
            for q in range(NPA):
                kq = NSAMP + q
                mcol = (d * NPO + o) * NPA + q
                stt(ini[:, d, s * 8:(s + 1) * 8], FB[:, d, kq * 8:(kq + 1) * 8], msk[:, mcol:mcol + 1], ini[:, d, s * 8:(s + 1) * 8],
                    ALU.mult, ALU.add, ["FB", "msk", "ini"], ["ini"])
    S.barrier()
    for s in range(NMAIN if cfg.get("stop") in (None, "apath", "b1", "b2", "b3", "b4") else 0):
        xi = s if s < NSAMP else NP1 + (s - NSAMP)
        if s == 0 or cfg.get("stop") is not None:
            load_x(xi)
        nxi = None
        if s + 1 < NMAIN and cfg.get("stop") is None:
            nxi = (s + 1) if (s + 1) < NSAMP else NP1 + (s + 1 - NSAMP)
        apath_seg(True, s)
        if cfg.get("barriers", 0):
            S.barrier()
        for t in range(TT if cfg.get("stop") != "apath" else 0):
            phase_b(s, xi, t, nxi)
        if cfg.get("barriers", 0):
            S.barrier()
    S.finish("sp")
    S.emit()
    return nc


def pieces_lhsT(W, ncols_per_piece=128):
    K, C = W.shape
    a = W.reshape(K // 128, 128, C // 128, 128)
    return np.ascontiguousarray(a.transpose(2, 1, 0, 3).reshape(C // 128, 128, (K // 128) * 128))


def prep_weights(inp):
    VO, NV = vec_layout()
    f = lambda a: np.ascontiguousarray(np.asarray(a, dtype=np.float32))
    w = {}
    pair = lambda a: np.ascontiguousarray(a.reshape(a.shape[0] // 2, 2, 128, a.shape[2]).transpose(0, 2, 1, 3).reshape(a.shape[0] // 2, 128, 2 * a.shape[2]))
    p_in0 = pieces_lhsT(f(inp["even_w_in"][0]))
    w["wA"] = np.ascontiguousarray(np.concatenate([p_in0[g * 8:(g + 1) * 8] for g in range(5)], axis=2))
    w["wO0"] = pair(pieces_lhsT(f(inp["even_w_out"][0])))
    pg = np.concatenate([pieces_lhsT(f(inp["ple_gate_w"][li])) for li in range(2)], 0)
    pl = np.concatenate([pieces_lhsT(f(inp["ple_w"][li])) for li in range(2)], 0)
    w["wPP"] = pair(np.concatenate([pg, pl], axis=2))
    wi = f(inp["odd_w_in"][0])
    w["wUG"] = pair(np.concatenate([pieces_lhsT(wi[:, 0:2048]), pieces_lhsT(wi[:, 4096:6144])], axis=2))
    wv = wi[:, 2048:4096].reshape(2, 4, 128, 4, 512)
    w["wV"] = pair(np.ascontiguousarray(wv.transpose(3, 0, 2, 1, 4).reshape(8, 128, 2048)))
    w["wO1"] = pair(pieces_lhsT(f(inp["odd_w_out"][0])))
    w["sguT"] = np.ascontiguousarray(f(inp["sgu_w"][0]).transpose(2, 0, 1))
    w["gw"] = np.ascontiguousarray(np.stack([f(inp["lru_r_w"][0][0]), f(inp["lru_i_w"][0][0]),
                                             f(inp["lru_r_w"][0][1]), f(inp["lru_i_w"][0][1])], 0))
    vec = np.zeros((128, NV), np.float32)
    pc = lambda v: f(v).reshape(-1, 128).T
    caw = f(inp["conv_a_w"][0])
    vec[:, VO["caw"]:VO["caw"] + 32] = caw.reshape(4, 8, 128).transpose(2, 1, 0).reshape(128, 32)
    vec[:, VO["cab"]:VO["cab"] + 8] = pc(inp["conv_a_b"][0])
    gb = [inp["lru_r_b"][0][0], inp["lru_i_b"][0][0], inp["lru_r_b"][0][1], inp["lru_i_b"][0][1]]
    for i, g in enumerate(gb):
        vec[:, VO["gb"] + i * 8:VO["gb"] + (i + 1) * 8] = pc(f(g).reshape(-1))
    for d in range(2):
        vec[:, VO["lam"] + d * 8:VO["lam"] + (d + 1) * 8] = pc(inp["lru_lam"][0][d])
    cbw = f(inp["conv_b_w"][0])
    vec[:, VO["cbw"]:VO["cbw"] + 248] = cbw.reshape(31, 8, 128).transpose(2, 1, 0).reshape(128, 248)
    vec[:, VO["cbb"]:VO["cbb"] + 8] = pc(inp["conv_b_b"][0])
    vec[:, VO["lnbg"]:VO["lnbg"] + 8] = pc(inp["lnb_g"][0])
    vec[:, VO["lnbb"]:VO["lnbb"] + 8] = pc(inp["lnb_b"][0])
    for li in range(2):
        vec[:, VO["lng"] + li * 8:VO["lng"] + (li + 1) * 8] = pc(inp["ln_g"][li])
        vec[:, VO["lnb"] + li * 8:VO["lnb"] + (li + 1) * 8] = pc(inp["ln_b"][li])
    vec[:, VO["lnvg"]:VO["lnvg"] + 16] = pc(inp["lnv_g"][0])
    w["vec"] = vec
    w["lnvb"] = f(inp["lnv_b"][0]).reshape(1, 2048)
    w["sgub"] = f(inp["sgu_b"][0]).reshape(1, 2048)
    w["ident"] = np.eye(128, dtype=np.float32)
    return w


def ext_segments(xseq, SL):
    Sq, D = xseq.shape
    xp = np.zeros((Sq + 30, D), np.float32)
    xp[15:15 + Sq] = xseq
    n = Sq // SL
    out = np.empty((n, D, SL + 30), np.float32)
    for i in range(n):
        out[i] = xp[i * SL:i * SL + SL + 30].T
    return out


def prep_core(cfg, x_samp, p_samp, x_prm, p_prm, own_start):
    SL = cfg["SL"]; NPA = cfg["n_prompt_all"]; NPO = cfg["n_prompt_own"]
    segs = [ext_segments(x_samp[i], SL) for i in range(x_samp.shape[0])]
    pr = ext_segments(x_prm, SL)
    xe = np.concatenate(segs + [pr, pr[own_start:own_start + NPO]], 0)
    pes = []
    for i in range(x_samp.shape[0]):
        ps_ = p_samp[:, i]
        for r in range(ps_.shape[1] // SL):
            pes.append(ps_[:, r * SL:(r + 1) * SL].transpose(0, 2, 1))
    for o in range(NPO):
        q = own_start + o
        pes.append(p_prm[:, q * SL:(q + 1) * SL].transpose(0, 2, 1))
    pe = np.ascontiguousarray(np.stack(pes, 0), dtype=np.float32)
    m = np.zeros((2, NPO, NPA), np.float32)
    for o in range(NPO):
        q = own_start + o
        if q - 1 >= 0:
            m[0, o, q - 1] = 1.0
        if q + 1 < NPA:
            m[1, o, q + 1] = 1.0
    msk = np.ascontiguousarray(np.broadcast_to(m.reshape(1, -1), (128, 2 * NPO * NPA)))
    return {"xe": np.ascontiguousarray(xe), "pe": pe, "msk": msk}


FULL = {"SL": 1024, "n_samp": 4, "segs_per_samp": 2, "n_prompt_all": 16, "n_prompt_own": 4}


def kernel(**inp):
    cfg = FULL
    SL = cfg["SL"]
    w = prep_weights(inp)
    xs = np.asarray(inp["x_sample"], np.float32); xp = np.asarray(inp["x_prompt"], np.float32)
    pss = np.asarray(inp["p_sample"], np.float32); pp = np.asarray(inp["p_prompt"], np.float32)
    in_maps = []
    for c in range(8):
        d = prep_core(cfg, xs[c * 4:(c + 1) * 4], pss[:, c * 4:(c + 1) * 4], xp[c // 4], pp[:, c // 4], (c % 4) * cfg["n_prompt_own"])
        d.update(w)
        in_maps.append(d)
    nc = build(cfg)
    res = run_bass_kernel_spmd(nc, in_maps, core_ids=list(range(8)))
    y_s = np.empty_like(xs); y_p = np.empty_like(xp)
    SPS = cfg["segs_per_samp"]; NPO = cfg["n_prompt_own"]
    for c in range(8):
        y = res.results[c]["y"]
        for i in range(4):
            for r in range(SPS):
                y_s[c * 4 + i, r * SL:(r + 1) * SL] = y[i * SPS + r].T
        for o in range(NPO):
            q = (c % 4) * NPO + o
            y_p[c // 4, q * SL:(q + 1) * SL] = y[4 * SPS + o].T
    return (y_p, y_s)
```

```python
import math
import numpy as np
import concourse.bass as bass
import concourse.mybir as mybir
from concourse.bass_utils import run_bass_kernel_spmd

F32 = mybir.dt.float32
BF16 = mybir.dt.bfloat16
AF = mybir.ActivationFunctionType
ALU = mybir.AluOpType

DM = 1024
NCH = 8
ALPHA = 4 ** 0.25
LN_EPS = 1e-5
EPOCH = 20000


class Sched:
    ENG = ("pe", "act", "dve", "pool", "sp")

    def __init__(self, nc):
        self.nc = nc
        self.prog = {e: [] for e in self.ENG}
        self.count = {e: 0 for e in self.ENG}
        self.esem = {e: [] for e in self.ENG}
        self.waited = {e: {} for e in self.ENG}
        self.lastw = {}
        self.readers = {}
        self.dsem = {}
        self.nsem = 0
        self.alias = {}

    def _x(self, names):
        out = list(names)
        for n in names:
            out.extend(self.alias.get(n, ()))
        return out

    def _newsem(self, name):
        self.nsem += 1
        return self.nc.alloc_semaphore(name=name)

    def _eng_token(self, e):
        k = self.count[e]
        ep = k // EPOCH
        while len(self.esem[e]) <= ep:
            self.esem[e].append(self._newsem(f"s_{e}_{len(self.esem[e])}"))
        self.count[e] = k + 1
        return (self.esem[e][ep], k % EPOCH + 1, e)

    def _emit_waits(self, e, toks):
        w = self.waited[e]
        need = {}
        for t in toks:
            if t is None:
                continue
            sem, val, te = t
            if te == "pe" and e == "pe":
                continue
            key = id(sem)
            if w.get(key, 0) >= val:
                continue
            if key not in need or need[key][1] < val:
                need[key] = (sem, val)
        for key, (sem, val) in need.items():
            w[key] = val
            self.prog[e].append(lambda eng, sem=sem, val=val: eng.wait_ge(sem, val))

    def _deps(self, reads, writes):
        reads = self._x(reads); writes = self._x(writes)
        toks = []
        for b in reads:
            toks.append(self.lastw.get(b))
        for b in writes:
            toks.append(self.lastw.get(b))
            toks.extend(self.readers.get(b, ()))
        return toks

    def _commit(self, tok, reads, writes):
        reads = self._x(reads); writes = self._x(writes)
        for b in reads:
            self.readers.setdefault(b, []).append(tok)
        for b in writes:
            self.lastw[b] = tok
            self.readers[b] = []

    def op(self, e, fn, reads=(), writes=()):
        self._emit_waits(e, self._deps(reads, writes))
        tok = self._eng_token(e)
        sem = tok[0]
        self.prog[e].append(lambda eng, fn=fn, sem=sem: fn(eng).then_inc(sem, 1))
        self._commit(tok, reads, writes)

    def group(self, e, fns, reads=(), writes=()):
        self._emit_waits(e, self._deps(reads, writes))
        tok = self._eng_token(e)
        sem = tok[0]
        for fn in fns[:-1]:
            self.prog[e].append(lambda eng, fn=fn: fn(eng))
        self.prog[e].append(lambda eng, fn=fns[-1], sem=sem: fn(eng).then_inc(sem, 1))
        self._commit(tok, reads, writes)

    def dma(self, q, out, in_, reads=(), writes=(), key=None):
        self._emit_waits(q, self._deps(reads, writes))
        if key is None:
            key = (writes[0] if writes else reads[0])
        if key not in self.dsem:
            self.dsem[key] = [self._newsem(f"d_{self.nsem}"), 0]
        ent = self.dsem[key]
        if ent[1] + 16 > 30000:
            ent[0] = self._newsem(f"d_{self.nsem}")
            ent[1] = 0
        ent[1] += 16
        tok = (ent[0], ent[1], None)
        sem = ent[0]
        self.prog[q].append(lambda eng, out=out, in_=in_, sem=sem: eng.dma_start(out=out, in_=in_).then_inc(sem, 16))
        self._commit(tok, reads, writes)

    def barrier(self):
        toks = []
        for e in ("pe", "act", "dve", "pool", "sp"):
            k = self.count[e]
            if k > 0:
                toks.append((self.esem[e][(k - 1) // EPOCH], (k - 1) % EPOCH + 1, "x" + e))
        for ent in self.dsem.values():
            toks.append((ent[0], ent[1], None))
        for e in self.ENG:
            self._emit_waits(e, toks)

    def finish(self, e="sp"):
        self._emit_waits(e, list(self.lastw.values()))

    def emit(self):
        with self.nc.Block() as block:
            @block.tensor
            def _(eng):
                for f in self.prog["pe"]:
                    f(eng)

            @block.scalar
            def _(eng):
                for f in self.prog["act"]:
                    f(eng)

            @block.vector
            def _(eng):
                for f in self.prog["dve"]:
                    f(eng)

            @block.gpsimd
            def _(eng):
                for f in self.prog["pool"]:
                    f(eng)

            @block.sync
            def _(eng):
                for f in self.prog["sp"]:
                    f(eng)


def vec_layout():
    off = {}
    n = 0
    for name, w in [("caw", 32), ("cab", 8), ("gb", 32), ("lam", 16), ("cbw", 248), ("cbb", 8), ("lnbg", 8),
                    ("lnbb", 8), ("lng", 16), ("lnb", 16), ("lnvg", 16)]:
        off[name] = n
        n += w
    return off, n


def build(cfg):
    SL = cfg["SL"]; TT = SL // 512; SE = SL + 30
    NSS = cfg["n_samp"]; SPS = cfg["segs_per_samp"]; NPA = cfg["n_prompt_all"]; NPO = cfg["n_prompt_own"]
    NSAMP = NSS * SPS
    NMAIN = NSAMP + NPO
    NP1 = NSAMP + NPA
    VO, NV = vec_layout()
    NRING = cfg.get("nring", 4)

    nc = bass.Bass("TRN2", target_bir_lowering=False)
    dt = lambda name, shape, kind="ExternalInput": nc.dram_tensor(name, shape, F32, kind=kind).ap()
    xe = dt("xe", [NP1 + NPO, DM, SE])
    pe_ = dt("pe", [NMAIN, 2, 256, SL])
    msk_d = dt("msk", [128, 2 * NPO * NPA])
    wA = dt("wA", [8, 128, 5120])
    wO0 = dt("wO0", [4, 128, 4096])
    wPP = dt("wPP", [8, 128, 2560])
    wUG = dt("wUG", [8, 128, 4096])
    wV = dt("wV", [4, 128, 4096])
    wO1 = dt("wO1", [4, 128, 4096])
    sguT_d = dt("sguT", [128, 16, 128])
    gw_d = dt("gw", [4, 16, 64, 64])
    vec_d = dt("vec", [128, NV])
    lnvb_d = dt("lnvb", [1, 2048])
    sgub_d = dt("sgub", [1, 2048])
    ident_d = dt("ident", [128, 128])
    y_d = dt("y", [NMAIN, DM, SL], kind="ExternalOutput")

    S = Sched(nc)
    sb = lambda name, shape, d=F32: nc.alloc_sbuf_tensor("sb_" + name, shape, d)
    ring = sb("ring", [128, NRING, 4096], BF16)
    cda = sb("cda", [128, 32, 128], BF16)
    gm = sb("gm", [128, 32, 128], BF16)
    WT = sb("WT", [128, 16, 128], BF16)
    Csb = sb("Csb", [128, 16, 128])
    ident_f = sb("ident_f", [128, 128])
    ident_b = sb("ident_b", [128, 128], BF16)
    onesM = sb("onesM", [128, 128], BF16)
    ones1 = sb("ones1", [1, 128])
    vec = sb("vec", [128, NV])
    cf = sb("cf", [128, 80])
    msk = sb("msk_sb", [128, 2 * NPO * NPA])
    EP = sb("EP", [128, 4, NP1 * 8])
    FB = sb("FB", [128, 2, NP1 * 8])
    ini = sb("ini", [128, 2, NMAIN * 8])
    racc = sb("racc", [128, 2 * TT])
    rsum = sb("rsum", [128, 2])
    xbf = sb("xbf", [128, 8, SE], BF16)
    xtail = sb("xtail", [128, 8, 4], BF16)
    ya = sb("ya", [128, 8, SL], BF16)
    ug = sb("ug", [128, 16, 512], BF16)
    vn = sb("vn", [128, 4, 2048], BF16)
    dg = sb("dg", [128, 2, 31 * 128], BF16)
    hbx = sb("hbx", [128, 2, 544], BF16)
    sgt = sb("sgt", [128, 2, 544], BF16)
    hbc = sb("hbc", [128, 8, 512], BF16)
    yb = sb("yb", [128, 8, 512], BF16)
    tq = sb("tq", [128, 2, 512], BF16)
    tb = sb("tb", [128, 2, 512], BF16)
    tf = sb("tf", [128, 2, 512])
    st_mu = sb("st_mu", [128, 512])
    st_rs = sb("st_rs", [128, 512])
    st_t = sb("st_t", [128, 512])
    xr = sb("xr", [128, 2, 512])
    s_ = sb("s_", [128, 8, 512])
    pbf = sb("pbf", [128, 2, 512], BF16)
    bst = sb("bst", [128, 4, 24])
    mv = sb("mv", [128, 4, 2])
    nrm = sb("nrm", [128, 4, 2])
    xlnb = hbc
    x1b = yb
    svt = tf
    assert 2 * SL <= 2048 and 4 * TT <= 8
    xa_b = [vn[:, 0, 0:SL + 8], dg[:, 0, 0:SL + 8]]
    xc_b = [vn[:, 1, 0:SL], vn[:, 1, SL:2 * SL]]
    mb_b = [vn[:, 2, 0:SL], vn[:, 2, SL:2 * SL]]
    sga_b = [vn[:, 3, 0:SL], vn[:, 3, SL:2 * SL]]
    thr = ug[:, 0:2 * TT, :].rearrange("p (d t) n -> p d (t n)", d=2)
    thi = ug[:, 2 * TT:4 * TT, :].rearrange("p (d t) n -> p d (t n)", d=2)
    Fh = s_[:, 0:2 * TT, :].rearrange("p (d t) n -> p d (t n)", d=2)
    Fa_b = [s_[:, 2 * TT + d * TT:2 * TT + (d + 1) * TT, :].rearrange("p t n -> p (t n)") for d in range(2)]
    Fu_b = [xr[:, 0:TT, :].rearrange("p t n -> p (t n)"), tf[:, 0:TT, :].rearrange("p t n -> p (t n)")]
    ps = nc.alloc_psum_tensor("ps", [128, 8, 512], F32)

    for p_ in range(2):
        for t_ in range(TT + 1):
            S.alias[("xa", p_, t_)] = [("vn", 0, t_)] if p_ == 0 else [("dg", 0)]
        for t_ in range(TT):
            S.alias[("xc", p_, t_)] = [("vn", 1, p_ * TT + t_)]
            S.alias[("sga", p_, t_)] = [("vn", 3, p_ * TT + t_)]
    for d_ in range(2):
        S.alias[("mb", d_)] = [("vn", 2, d_ * TT + t_) for t_ in range(TT)]
        S.alias[("Fh", d_)] = [("s", d_ * TT + t_) for t_ in range(TT)]
        S.alias[("Fa", d_)] = [("s", 2 * TT + d_ * TT + t_) for t_ in range(TT)]
        S.alias[("Fu", d_)] = [("xr", 0), ("xr", 1)] if d_ == 0 else [("tf", 0), ("tf", 1)]
        for t_ in range(TT):
            S.alias[("thr", d_, t_)] = [("ug", d_ * TT + t_)]
            S.alias[("thi", d_, t_)] = [("ug", 2 * TT + d_ * TT + t_)]
    st = {"ring": 0, "bank": 0, "tg": 0, "wa": [None, None, None, None]}

    def nb():
        b = st["bank"]
        st["bank"] = (b + 1) % 6
        return b

    def nb2():
        b = ((st["bank"] + 1) // 2 * 2) % 6
        st["bank"] = (b + 2) % 6
        return b

    MERGE = (TT == 2 and cfg.get("merge", 0))

    def ps2(b):
        return ps[:, b:b + 2, :].rearrange("p a n -> p (a n)")

    def wp(src, n):
        i = st["ring"]
        st["ring"] = (i + 1) % NRING
        S.dma("pool", ring[:, i, 0:n], src, writes=[("w", i)])
        return ring[:, i, :], ("w", i)

    def mm(out, pairs, reads, writes):
        n = len(pairs)
        fns = [(lambda e, l=l, r=r, i=i: e.matmul(out, lhsT=l, rhs=r, start=(i == 0), stop=(i == n - 1)))
               for i, (l, r) in enumerate(pairs)]
        S.group("pe", fns, reads=reads, writes=writes)

    def act(out, in_, func, reads, writes, scale=1.0, bias=0.0, eng="act"):
        S.op(eng, lambda e: e.activation(out=out, in_=in_, func=func, scale=scale, bias=bias), reads=reads, writes=writes)

    def tt(eng, out, a, b, op, reads, writes):
        S.op(eng, lambda e: e.tensor_tensor(out=out, in0=a, in1=b, op=op), reads=reads, writes=writes)

    def stt(out, a, scalar, b, op0, op1, reads, writes):
        S.op("dve", lambda e: e.scalar_tensor_tensor(out=out, in0=a, scalar=scalar, in1=b, op0=op0, op1=op1),
             reads=reads, writes=writes)

    def ts(eng, out, a, s1, s2, op0, op1, reads, writes):
        if s2 is None:
            S.op(eng, lambda e: e.tensor_scalar(out=out, in0=a, scalar1=s1, scalar2=None, op0=op0), reads=reads, writes=writes)
        else:
            S.op(eng, lambda e: e.tensor_scalar(out=out, in0=a, scalar1=s1, scalar2=s2, op0=op0, op1=op1), reads=reads, writes=writes)

    POOLE = cfg.get("pool_eng", "dve")
    V = lambda name, i: vec[:, VO[name] + i: VO[name] + i + 1]

    p1first = 1 if NSAMP >= 2 else 0
    if cfg.get("stop") != "prologue":
        for kc in range(8):
            S.dma("pool", xbf[:, kc, :], xe[p1first, kc * 128:(kc + 1) * 128, :], writes=["xbf"], key="xbfdma")
    S.dma("sp", vec[:], vec_d, writes=["vec"])
    S.dma("sp", ident_f[:], ident_d, writes=["ident_f"])
    S.dma("sp", msk[:], msk_d, writes=["msk"])
    S.dma("sp", s_[:, 4:8, :], sguT_d.rearrange("p (a b) q -> p a (b q)", b=4), writes=["WT32"])
    S.dma("sp", xr[0:1, :, :], sgub_d[:, 0:1024].rearrange("o (a b) -> o a b", b=512), writes=["sgub0"])
    S.dma("sp", tf[0:1, :, :], sgub_d[:, 1024:2048].rearrange("o (a b) -> o a b", b=512), writes=["sgub1"])
    S.dma("sp", s_[:, 0:4, :], lnvb_d.partition_broadcast(128).rearrange("p o (a b) -> p (o a) b", b=512), writes=["lnvb"])
    S.dma("pool", WT[:], sguT_d, writes=["WT"])
    S.dma("pool", ident_b[:], ident_d, writes=["ident_b"])
    S.op("dve", lambda e: e.memset(gm[:], 0.0), writes=["gm"])
    S.op("dve", lambda e: e.memset(onesM[:], 1.0 / 1024), writes=["onesM"])
    S.op("dve", lambda e: e.memset(ones1[:], 1.0), writes=["ones1"])
    for kind in range(4):
        for h in range(16):
            j, hh = h // 2, h % 2
            S.dma("pool", gm[hh * 64:(hh + 1) * 64, kind * 8 + j, hh * 64:(hh + 1) * 64], gw_d[kind, h], reads=["gm"], writes=[("gmb", (kind * 16 + h) % 8)], key="gmdma")
    for j in range(8):
        for k in range(4):
            act(cda[:, j * 4 + k, :], ident_f[:], AF.Copy, ["ident_f", "vec"], [("cda", j, k)], scale=V("caw", j * 4 + k))
    lam = vec[:, VO["lam"]:VO["lam"] + 16]
    act(cf[:, 0:16], lam, AF.Exp, ["vec"], ["cf0"], scale=-1.0)
    act(cf[:, 0:16], cf[:, 0:16], AF.Ln, ["cf0"], ["cf0"], bias=1.0)
    ts("dve", cf[:, 16:32], cf[:, 0:16], -4.0, None, ALU.mult, None, ["cf0"], ["cf1"])
    ts("dve", cf[:, 0:16], cf[:, 0:16], -8.0, None, ALU.mult, None, ["cf0", "cf1"], ["cf0"])
    ts("dve", cf[:, 32:64], vec[:, VO["gb"]:VO["gb"] + 32], 0.5, None, ALU.mult, None, ["vec"], ["cf2"])
    ts("dve", cf[:, 64:80], cf[:, 16:32], float(SL), None, ALU.mult, None, ["cf1"], ["cf3"])
    CF = ["cf0", "cf1", "cf2", "cf3"]
    for h in range(16):
        b = nb()
        sgs = (xr if h < 8 else tf)[0:1, (h % 8) // 4, (h % 4) * 128:(h % 4 + 1) * 128]
        mm(ps[:, b, 0:128], [(s_[:, h // 4, (h % 4) * 128:(h % 4 + 1) * 128], s_[:, 4 + h // 4, (h % 4) * 128:(h % 4 + 1) * 128]), (ones1[:], sgs)],
           ["lnvb", "WT32", "ones1", "sgub0", "sgub1"], [("ps", b)])
        act(Csb[:, h, :], ps[:, b, 0:128], AF.Copy, [("ps", b)], ["Csb"])

    S.barrier()

    def load_x(xi, alt=False):
        for kc in range(8):
            if not alt:
                S.dma("pool", xbf[:, kc, :], xe[xi, kc * 128:(kc + 1) * 128, :], writes=["xbf"], key="xbfdma")
            else:
                S.dma("pool", ya[:, kc, :], xe[xi, kc * 128:(kc + 1) * 128, 13:13 + SL], writes=["xalt"], key="xaltdma")
                S.dma("pool", xtail[:, kc, 0:3], xe[xi, kc * 128:(kc + 1) * 128, 13 + SL:16 + SL], writes=["xalt"], key="xaltdma")

    def xsrc(alt, kc, e0, N):
        if not alt:
            return xbf[:, kc, e0:e0 + N]
        if e0 == 13 + SL:
            return xtail[:, kc, 0:N]
        return ya[:, kc, e0 - 13:e0 - 13 + N]

    def ap1(j, alt, main):
        xa = xa_b[j % 2]; xn = "xalt" if alt else "xbf"
        if j + 2 < 8:
            st["wa"][(j + 2) % 4] = wp(wA[j + 2, :, 0:(2048 if main else 1024)], 2048 if main else 1024)
        wxa, nxa = st["wa"][j % 4]
        b2 = nb2() if MERGE else None
        for t in range(TT + 1):
            e0 = 13 + 512 * t
            N = 512 if t < TT else 3
            b = (b2 + t) if (MERGE and t < TT) else nb()
            mm(ps[:, b, 0:N], [(wxa[:, kc * 128:(kc + 1) * 128], xsrc(alt, kc, e0, N)) for kc in range(8)], [nxa, xn], [("ps", b)])
            if MERGE and t < TT:
                if t == TT - 1:
                    act(xa[:, 0:SL], ps2(b2), AF.Copy, [("ps", b2), ("ps", b2 + 1)], [("xa", j % 2, 0), ("xa", j % 2, 1)])
            else:
                act(xa[:, 512 * t:512 * t + N], ps[:, b, 0:N], AF.Copy, [("ps", b)], [("xa", j % 2, t)])

    def ap2(j, main, dirs):
        xa = xa_b[j % 2]; xc = xc_b[j % 2]; sga = sga_b[j % 2]; p = j % 2
        b2 = nb2() if MERGE else None
        for t in range(TT):
            b = (b2 + t) if MERGE else nb()
            mm(ps[:, b, :], [(cda[:, j * 4 + k, :], xa[:, 512 * t + k:512 * t + k + 512]) for k in range(4)],
               [("xa", p, t), ("xa", p, t + 1)] + [("cda", j, k) for k in range(4)], [("ps", b)])
            if not MERGE:
                act(xc[:, 512 * t:512 * (t + 1)], ps[:, b, :], AF.Identity, [("ps", b), "vec"], [("xc", p, t)], bias=V("cab", j))
        if MERGE:
            act(xc, ps2(b2), AF.Identity, [("ps", b2), ("ps", b2 + 1), "vec"], [("xc", p, 0), ("xc", p, 1)], bias=V("cab", j))
        if main:
            wslot, nga = st["wa"][j % 4]
            wga = wslot[:, 1024:2048]
        if MERGE:
            for d in dirs:
                for gi, dst, nm in ((0, thr, "thr"), (1, thi, "thi")):
                    b2 = nb2()
                    for t in range(TT):
                        mm(ps[:, b2 + t, :], [(gm[:, (d * 2 + gi) * 8 + j, :], xc[:, 512 * t:512 * (t + 1)])], ["gm", ("xc", p, t)], [("ps", b2 + t)])
                    bia = cf[:, 32 + (d * 2 + gi) * 8 + j:32 + (d * 2 + gi) * 8 + j + 1]
                    wr = [(nm, d, t) for t in range(TT)]
                    if gi == 0 and not main:
                        S.op("act", lambda e, dst=dst, d=d, b2=b2, bia=bia: e.activation(
                            out=dst[:, d, :], in_=ps2(b2), func=AF.Tanh, scale=0.5, bias=bia,
                            accum_out=racc[:, d * TT:d * TT + 1]), reads=[("ps", b2), ("ps", b2 + 1)] + CF, writes=wr + [("racc", d, 0)])
                    else:
                        act(dst[:, d, :], ps2(b2), AF.Tanh, [("ps", b2), ("ps", b2 + 1)] + CF, wr, scale=0.5, bias=bia)
            if main:
                b2 = nb2()
                for t in range(TT):
                    mm(ps[:, b2 + t, :], [(wga[:, kc * 128:(kc + 1) * 128], xbf[:, kc, 15 + 512 * t:15 + 512 * (t + 1)]) for kc in range(8)],
                       [nga, "xbf"], [("ps", b2 + t)])
                act(sga, ps2(b2), AF.Silu, [("ps", b2), ("ps", b2 + 1)], [("sga", p, t) for t in range(TT)])
            return
        for t in range(TT):
            sl = slice(512 * t, 512 * (t + 1))
            for d in dirs:
                for gi, dst, nm in ((0, thr, "thr"), (1, thi, "thi")):
                    b = nb()
                    mm(ps[:, b, :], [(gm[:, (d * 2 + gi) * 8 + j, :], xc[:, sl])], ["gm", ("xc", p, t)], [("ps", b)])
                    bia = cf[:, 32 + (d * 2 + gi) * 8 + j:32 + (d * 2 + gi) * 8 + j + 1]
                    if gi == 0 and not main:
                        S.op("act", lambda e, dst=dst, d=d, sl=sl, b=b, bia=bia, t=t: e.activation(
                            out=dst[:, d, sl], in_=ps[:, b, :], func=AF.Tanh, scale=0.5, bias=bia,
                            accum_out=racc[:, d * TT + t:d * TT + t + 1]), reads=[("ps", b)] + CF, writes=[(nm, d, t), ("racc", d, t)])
                    else:
                        act(dst[:, d, sl], ps[:, b, :], AF.Tanh, [("ps", b)] + CF, [(nm, d, t)], scale=0.5, bias=bia)
            if main:
                b = nb()
                mm(ps[:, b, :], [(wga[:, kc * 128:(kc + 1) * 128], xbf[:, kc, 15 + 512 * t:15 + 512 * (t + 1)]) for kc in range(8)],
                   [nga, "xbf"], [("ps", b)])
                act(sga[:, sl], ps[:, b, :], AF.Silu, [("ps", b)], [("sga", p, t)])

    def ap3(j, main, idx, dirs):
        xc = xc_b[j % 2]; sga = sga_b[j % 2]; p = j % 2
        for d in dirs:
            chalf = cf[:, 16 + d * 8 + j:16 + d * 8 + j + 1]
            act(Fa_b[d], thr[:, d, :], AF.Exp, [("thr", d, t) for t in range(TT)] + CF, [("Fa", d)], scale=chalf, bias=chalf)
            tt(POOLE, Fu_b[d], Fa_b[d], Fa_b[d], ALU.mult, [("Fa", d)], [("Fu", d)])
        for d in dirs:
            act(Fu_b[d], Fu_b[d], AF.Ln, [("Fu", d)], [("Fu", d)], scale=-1.0, bias=1.0 + 1e-6)
            act(mb_b[d], Fu_b[d], AF.Exp, [("Fu", d)], [("mb", d)], scale=0.5, bias=math.log(0.5))
            if not main:
                c0 = idx * 8 + j
                if TT == 2 and not MERGE:
                    tt("dve", rsum[:, d:d + 1], racc[:, d * TT:d * TT + 1], racc[:, d * TT + 1:d * TT + 2], ALU.add,
                       [("racc", d, 0), ("racc", d, 1)], [("rsum", d)])
                    rs_ap, rs_n = rsum[:, d:d + 1], ("rsum", d)
                else:
                    rs_ap, rs_n = racc[:, d * TT:d * TT + 1], ("racc", d, 0)
                chalf = cf[:, 16 + d * 8 + j:16 + d * 8 + j + 1]
                act(EP[:, 2 + d, c0:c0 + 1], rs_ap, AF.Exp, [rs_n] + CF, [("EP", 2 + d)], scale=chalf, bias=cf[:, 64 + d * 8 + j:64 + d * 8 + j + 1])
        for d in dirs:
            Fa = Fa_b[d]; Fu = Fu_b[d]
            stt(Fu, thi[:, d, :], 1.0, xc, ALU.add, ALU.mult, [("thi", d, t) for t in range(TT)] + [("xc", p, t) for t in range(TT)] + [("mb", d)], [("Fu", d)])
            tt(POOLE, Fu, Fu, mb_b[d], ALU.mult, [("Fu", d), ("mb", d)], [("Fu", d)])
            if main:
                init = ini[:, d, idx * 8 + j:idx * 8 + j + 1]
                rinit = ["ini"]
            else:
                init = 0.0
                rinit = []
            if d == 0:
                S.op("dve", lambda e, init=init, Fa=Fa, Fu=Fu: e.tensor_tensor_scan(out=Fh[:, 0, :], data0=Fa, data1=Fu, initial=init,
                                                                                   op0=ALU.mult, op1=ALU.add), reads=[("Fa", d), ("Fu", d)] + rinit, writes=[("Fh", 0)])
            else:
                S.op("dve", lambda e, init=init, Fa=Fa, Fu=Fu: e.tensor_tensor_scan(out=Fh[:, 1, ::-1], data0=Fa[:, ::-1], data1=Fu[:, ::-1], initial=init,
                                                                                   op0=ALU.mult, op1=ALU.add), reads=[("Fa", d), ("Fu", d)] + rinit, writes=[("Fh", 1)])
            if not main:
                col = SL - 1 if d == 0 else 0
                c0 = idx * 8 + j
                S.op("dve", lambda e, d=d, col=col, c0=c0: e.tensor_copy(out=EP[:, d, c0:c0 + 1], in_=Fh[:, d, col:col + 1]),
                     reads=[("Fh", d)], writes=[("EP", d)])
            elif d == 0 and ((idx < NSAMP and idx % 2 == 0) or (NSAMP <= idx < NMAIN - 1)):
                c1 = (idx + 1) * 8 + j
                S.op("dve", lambda e, c1=c1: e.tensor_copy(out=ini[:, 0, c1:c1 + 1], in_=Fh[:, 0, SL - 1:SL]),
                     reads=[("Fh", 0), "ini"], writes=["ini"])
        if main:
            tt(POOLE, Fh[:, 0, :], Fh[:, 0, :], Fh[:, 1, :], ALU.add, [("Fh", 0), ("Fh", 1)], [("Fh", 0)])
            tt(POOLE, ya[:, j, :], Fh[:, 0, :], sga, ALU.mult, [("Fh", 0)] + [("sga", p, t) for t in range(TT)], [("ya", j)])

    def apath_seg(main, idx, alt=False, dirs=(0, 1)):
        for j0 in range(2):
            st["wa"][j0] = wp(wA[j0, :, 0:(2048 if main else 1024)], 2048 if main else 1024)
        ap1(0, alt, main)
        for j in range(8):
            if j + 1 < 8:
                ap1(j + 1, alt, main)
            ap2(j, main, dirs)
            ap3(j, main, idx, dirs)

    def ln_stats_finish():
        act(st_mu[:], ps[:, 6, :], AF.Copy, [("ps", 6)], ["st_mu"])
        act(st_t[:], ps[:, 6, :], AF.Square, [("ps", 6)], ["st_t"])
        tt("dve", st_t[:], ps[:, 7, :], st_t[:], ALU.subtract, [("ps", 7), "st_t"], ["st_t"])
        ts("dve", st_t[:], st_t[:], LN_EPS, None, ALU.add, None, ["st_t"], ["st_t"])
        act(st_t[:], st_t[:], AF.Ln, ["st_t"], ["st_t"])
        act(st_rs[:], st_t[:], AF.Exp, ["st_t"], ["st_rs"], scale=-0.5)

    def stats_acc(src_bf, m, rd):
        i = st["tg"]; st["tg"] ^= 1
        act(tq[:, i, :], src_bf, AF.Square, rd, [("tq", i)])
        mm(ps[:, 6, :], [(onesM[:], src_bf)], rd + ["onesM"], [("ps", 6)]) if False else None
        S.group("pe", [lambda e: e.matmul(ps[:, 6, :], lhsT=onesM[:], rhs=src_bf, start=(m == 0), stop=(m == 7))],
                reads=rd + ["onesM"], writes=[("ps", 6)])
        S.group("pe", [lambda e: e.matmul(ps[:, 7, :], lhsT=onesM[:], rhs=tq[:, i, :], start=(m == 0), stop=(m == 7))],
                reads=[("tq", i), "onesM"], writes=[("ps", 7)])

    def ln_ple(li, s, t0, pull=lambda n: None):
        for m in range(8):
            i = st["tg"]
            act(tb[:, i, :], s_[:, m, :], AF.Copy, [("s", m)], [("tb", i)])
            stats_acc(tb[:, i, :], m, [("tb", i)])
        pull(1)
        ln_stats_finish()
        S.dma("pool", pbf[:], pe_[s, li, :, t0:t0 + 512].rearrange("(a p) n -> p a n", p=128), writes=["pbf"])
        for m in range(8):
            if m in (3, 6):
                pull(1)
            tt("dve", s_[:, m, :], s_[:, m, :], st_mu[:], ALU.subtract, [("s", m), "st_mu"], [("s", m)])
            tt("dve", s_[:, m, :], s_[:, m, :], st_rs[:], ALU.mult, [("s", m), "st_rs"], [("s", m)])
            act(s_[:, m, :], s_[:, m, :], AF.Identity, [("s", m), "vec"], [("s", m)], scale=V("lng", li * 8 + m), bias=V("lnb", li * 8 + m))
            act(xlnb[:, m, :], s_[:, m, :], AF.Copy, [("s", m)], [("hbc", m)])
        for m in range(8):
            if m % 2 == 0:
                wpp, ng = wp(wPP[li * 4 + m // 2], 2560)
            wg = wpp[:, (m % 2) * 1280:(m % 2) * 1280 + 1024]
            wl = wpp[:, (m % 2) * 1280 + 1024:(m % 2 + 1) * 1280]; nl = ng
            b = nb()
            mm(ps[:, b, :], [(wg[:, kc * 128:(kc + 1) * 128], xlnb[:, kc, :]) for kc in range(8)],
               [ng] + [("hbc", kc) for kc in range(8)], [("ps", b)])
            b2 = nb()
            mm(ps[:, b2, :], [(wl[:, kc * 128:(kc + 1) * 128], pbf[:, kc, :]) for kc in range(2)], [nl, "pbf"], [("ps", b2)])
            i = st["tg"]; st["tg"] ^= 1
            act(tf[:, i, :], ps[:, b, :], AF.Tanh, [("ps", b)], [("tf", i)], scale=0.5)
            stt(tf[:, i, :], tf[:, i, :], 1.0, ps[:, b2, :], ALU.add, ALU.mult, [("tf", i), ("ps", b2)], [("tf", i)])
            stt(s_[:, m, :], tf[:, i, :], 0.5, s_[:, m, :], ALU.mult, ALU.add, [("tf", i), ("s", m)], [("s", m)])
            if li == 0:
                act(x1b[:, m, :], s_[:, m, :], AF.Copy, [("s", m)], [("yb", m)])

    def phase_b(s, xi, t, next_xi=None):
        t0 = 512 * t
        hoist_out = (TT == 2 and t == 0 and cfg.get("hoist", 1))
        hoisted_in = (TT == 2 and t == 1 and cfg.get("hoist", 1))

        def conv_steps(tb0, dst, dname):
            def bs1(j):
                i = j % 2
                S.op("dve", lambda e, i=i, j=j: e.tensor_tensor(
                    out=dg[:, i, :].rearrange("p (k c) -> p k c", c=128),
                    in0=ident_b[:].unsqueeze(1).broadcast_to([128, 31, 128]),
                    in1=vec[:, VO["cbw"] + j * 31:VO["cbw"] + (j + 1) * 31].unsqueeze(2).broadcast_to([128, 31, 128]),
                    op=ALU.mult), reads=["ident_b", "vec"], writes=[("dg", i)])
                wab, na = wp(wA[j, :, 2048:4096], 2048)
                wa = wab[:, 0:1024]; wb_ = wab[:, 1024:2048]
                for (c0, N) in ((0, 512), (512, 30)):
                    ba = nb()
                    mm(ps[:, ba, 0:N], [(wa[:, kc * 128:(kc + 1) * 128], xbf[:, kc, tb0 + c0:tb0 + c0 + N]) for kc in range(8)], [na, "xbf"], [("ps", ba)])
                    bb = nb()
                    mm(ps[:, bb, 0:N], [(wb_[:, kc * 128:(kc + 1) * 128], xbf[:, kc, tb0 + c0:tb0 + c0 + N]) for kc in range(8)], [na, "xbf"], [("ps", bb)])
                    act(sgt[:, i, c0:c0 + N], ps[:, bb, 0:N], AF.Tanh, [("ps", bb)], [("sgt", i, c0)], scale=0.5)
                    stt(hbx[:, i, c0:c0 + N], sgt[:, i, c0:c0 + N], 1.0, ps[:, ba, 0:N], ALU.add, ALU.mult,
                        [("sgt", i, c0), ("ps", ba)], [("hbx", i, c0)])

            def bs2(j):
                i = j % 2
                b = nb()
                mm(ps[:, b, :], [(dg[:, i, k * 128:(k + 1) * 128], hbx[:, i, k:k + 512]) for k in range(31)],
                   [("dg", i), ("hbx", i, 0), ("hbx", i, 512)], [("ps", b)])
                act(dst(j), ps[:, b, :], AF.Identity, [("ps", b), "vec"], [(dname, j)], scale=0.5, bias=V("cbb", j))

            steps = [lambda: bs1(0)]
            for j in range(8):
                if j + 1 < 8:
                    steps.append(lambda j=j: (bs1(j + 1), bs2(j)))
                else:
                    steps.append(lambda j=j: bs2(j))
            return steps

        hoisted_in2 = (t == 0 and st.get("b0_hoisted"))
        if t == 0:
            st["b0_hoisted"] = False
        if hoisted_in:
            hsrc = lambda j: ya[:, j, 0:512]
            hname = "ya"
        elif hoisted_in2:
            hsrc = lambda j: yb[:, j, :]
            hname = "yb"
        else:
            hsrc = lambda j: hbc[:, j, :]
            hname = "hbc"
            for stp in conv_steps(t0, hsrc, hname):
                stp()
        for j in range(8):
            i = j % 2
            act(tq[:, i, :], hsrc(j), AF.Square, [(hname, j)], [("tq", i)])
            S.group("pe", [lambda e, j=j: e.matmul(ps[:, 6, :], lhsT=onesM[:], rhs=hsrc(j), start=(j == 0), stop=(j == 7))],
                    reads=[(hname, j), "onesM"], writes=[("ps", 6)])
            S.group("pe", [lambda e, j=j, i=i: e.matmul(ps[:, 7, :], lhsT=onesM[:], rhs=tq[:, i, :], start=(j == 0), stop=(j == 7))],
                    reads=[("tq", i), "onesM"], writes=[("ps", 7)])
        pend = conv_steps(512, lambda j: ya[:, j, 0:512], "ya") if hoist_out else []

        def pull(n):
            for _ in range(n):
                if pend:
                    pend.pop(0)()
        ln_stats_finish()
        for j in range(8):
            i = st["tg"]; st["tg"] ^= 1
            tt("dve", tf[:, i, :], hsrc(j), st_mu[:], ALU.subtract, [(hname, j), "st_mu"], [("tf", i)])
            tt("dve", tf[:, i, :], tf[:, i, :], st_rs[:], ALU.mult, [("tf", i), "st_rs"], [("tf", i)])
            act(tb[:, i, :], tf[:, i, :], AF.Silu, [("tf", i), "vec"], [("tb", i)], scale=V("lnbg", j), bias=V("lnbb", j))
            wg_, ng_ = wp(wA[j, :, 4096:5120], 1024)
            b = nb()
            mm(ps[:, b, :], [(wg_[:, kc * 128:(kc + 1) * 128], xbf[:, kc, 15 + t0:15 + t0 + 512]) for kc in range(8)], [ng_, "xbf"], [("ps", b)])
            act(tq[:, i, :], ps[:, b, :], AF.Silu, [("ps", b)], [("tq", i)])
            tt("dve", yb[:, j, :], tb[:, i, :], tq[:, i, :], ALU.mult, [("tb", i), ("tq", i)], [("yb", j)])
        hoist_next = (next_xi is not None and t == TT - 1 and cfg.get("hoist2", 1))
        if next_xi is not None and t == TT - 1:
            load_x(next_xi)
        if cfg.get("stop") == "b1":
            return
        for m in range(8):
            if m % 2 == 0:
                wo2, no = wp(wO0[m // 2], 4096)
            wo = wo2[:, (m % 2) * 2048:(m % 2 + 1) * 2048]
            b = nb()
            prs = [(wo[:, kc * 128:(kc + 1) * 128], ya[:, kc, t0:t0 + 512]) for kc in range(8)]
            prs += [(wo[:, (8 + kc) * 128:(9 + kc) * 128], yb[:, kc, :]) for kc in range(8)]
            mm(ps[:, b, :], prs, [no] + [("ya", kc) for kc in range(8)] + [("yb", kc) for kc in range(8)], [("ps", b)])
            S.dma("sp", xr[:, m % 2, :], xe[xi, m * 128:(m + 1) * 128, 15 + t0:15 + t0 + 512], writes=[("xr", m % 2)])
            stt(s_[:, m, :], xr[:, m % 2, :], ALPHA, ps[:, b, :], ALU.mult, ALU.add, [("xr", m % 2), ("ps", b)], [("s", m)])
        if cfg.get("stop") == "b2":
            return
        ln_ple(0, s, t0, pull if not hoist_next else (lambda n: None))
        if cfg.get("stop") == "b3":
            return
        for h in range(16):
            if h % 2 == 0:
                wug, nu = wp(wUG[h // 2], 4096)
            wu = wug[:, (h % 2) * 2048:(h % 2) * 2048 + 1024]
            wg2 = wug[:, (h % 2) * 2048 + 1024:(h % 2 + 1) * 2048]; ng2 = nu
            bu = nb()
            mm(ps[:, bu, :], [(wu[:, kc * 128:(kc + 1) * 128], x1b[:, kc, :]) for kc in range(8)], [nu] + [("yb", kc) for kc in range(8)], [("ps", bu)])
            bg = nb()
            mm(ps[:, bg, :], [(wg2[:, kc * 128:(kc + 1) * 128], x1b[:, kc, :]) for kc in range(8)], [ng2] + [("yb", kc) for kc in range(8)], [("ps", bg)])
            i = st["tg"]; st["tg"] ^= 1
            act(tb[:, i, :], ps[:, bg, :], AF.Silu, [("ps", bg)], [("tb", i)])
            tt("dve", ug[:, h, :], ps[:, bu, :], tb[:, i, :], ALU.mult, [("ps", bu), ("tb", i)], [("ug", h)])
        for cg in range(4):
            wv01, nv0 = wp(wV[cg], 4096)
            wv0 = wv01[:, 0:2048]; wv1 = wv01[:, 2048:4096]; nv1 = nv0
            for n in range(4):
                b = nb()
                prs = [(x1b[:, kc, n * 128:(n + 1) * 128], (wv0 if kc < 4 else wv1)[:, (kc % 4) * 512:(kc % 4 + 1) * 512]) for kc in range(8)]
                mm(ps[:, b, :], prs, [nv0, nv1] + [("yb", kc) for kc in range(8)], [("ps", b)])
                act(vn[:, n, cg * 512:(cg + 1) * 512], ps[:, b, :], AF.Copy, [("ps", b)], [("vn", n, cg)])
                S.op("dve", lambda e, n=n, cg=cg: e.bn_stats(out=bst[:, n, cg * 6:(cg + 1) * 6], in_=vn[:, n, cg * 512:(cg + 1) * 512]), reads=[("vn", n, cg)], writes=[("bst", n, cg)])
        if hoist_next:
            pend.extend(conv_steps(0, lambda j: yb[:, j, :], "yb"))
            st["b0_hoisted"] = True
        pull(1)
        for n in range(4):
            if n in (1, 3):
                pull(1)
            S.op("dve", lambda e, n=n: e.bn_aggr(out=mv[:, n, :], in_=bst[:, n, :]), reads=[("bst", n, cg) for cg in range(4)], writes=[("mv", n)])
            ts("dve", nrm[:, n, 0:1], mv[:, n, 1:2], LN_EPS, None, ALU.add, None, [("mv", n)], [("nrm", n)])
            act(nrm[:, n, 0:1], nrm[:, n, 0:1], AF.Ln, [("nrm", n)], [("nrm", n)])
            act(nrm[:, n, 0:1], nrm[:, n, 0:1], AF.Exp, [("nrm", n)], [("nrm", n)], scale=-0.5)
            tt("dve", nrm[:, n, 1:2], mv[:, n, 0:1], nrm[:, n, 0:1], ALU.mult, [("mv", n), ("nrm", n)], [("nrm2", n)])
            ts("dve", nrm[:, n, 1:2], nrm[:, n, 1:2], -1.0, None, ALU.mult, None, [("nrm2", n)], [("nrm2", n)])
            act(vn[:, n, :], vn[:, n, :], AF.Identity, [("vn", n, cg) for cg in range(4)] + [("nrm", n), ("nrm2", n)],
                [("vn", n, cg) for cg in range(4)], scale=nrm[:, n, 0:1], bias=nrm[:, n, 1:2])
        for h in range(16):
            if h in (2, 6, 10, 14):
                pull(1)
            b = nb()
            fns = [(lambda e, n=n, b=b, h=h: e.matmul(ps[:, b, n * 128:(n + 1) * 128], lhsT=vn[:, n, h * 128:(h + 1) * 128], rhs=WT[:, h, :],
                                                     start=True, stop=True)) for n in range(4)]
            S.group("pe", fns, reads=["WT"] + [("vn", n, h // 4) for n in range(4)], writes=[("ps", b)])
            i = st["tg"]; st["tg"] ^= 1
            stt(svt[:, i, :].rearrange("p (n q) -> p n q", q=128), ps[:, b, :].rearrange("p (n q) -> p n q", q=128), V("lnvg", h),
                Csb[:, h, :].unsqueeze(1).broadcast_to([128, 4, 128]), ALU.mult, ALU.add, [("ps", b), "Csb", "vec"], [("tf", i)])
            tt(cfg.get("sgu_eng", "dve"), ug[:, h, :], ug[:, h, :], svt[:, i, :], ALU.mult, [("ug", h), ("tf", i)], [("ug", h)])
        if cfg.get("stop") == "b4":
            return
        for m in range(8):
            if m % 2 == 0:
                wo2, no = wp(wO1[m // 2], 4096)
            wo = wo2[:, (m % 2) * 2048:(m % 2 + 1) * 2048]
            b = nb()
            mm(ps[:, b, :], [(wo[:, kc * 128:(kc + 1) * 128], ug[:, kc, :]) for kc in range(16)], [no] + [("ug", h) for h in range(16)], [("ps", b)])
            stt(s_[:, m, :], s_[:, m, :], ALPHA, ps[:, b, :], ALU.mult, ALU.add, [("s", m), ("ps", b)], [("s", m)])
        ln_ple(1, s, t0, pull)
        pull(99)
        for m in range(8):
            S.dma("sp", y_d[s, m * 128:(m + 1) * 128, t0:t0 + 512], s_[:, m, :], reads=[("s", m)], writes=[("yout", m)])

    assert SPS == 2
    p1list = [k for k in range(NP1) if not (k < NSAMP and k % 2 == 0)]
    if cfg.get("stop") == "prologue":
        p1list = []
    assert not p1list or p1list[0] == p1first
    for ii, k in enumerate(p1list):
        if ii + 1 < len(p1list):
            load_x(p1list[ii + 1], alt=((ii + 1) % 2 == 1))
        if k < NSAMP:
            dirs = (1,)
        elif k == NSAMP and NPA > 1:
            dirs = (0,)
        elif k - NSAMP >= NPA - NPO and NPA > 1:
            dirs = (1,)
        else:
            dirs = (0, 1)
        apath_seg(False, k, alt=(ii % 2 == 1), dirs=dirs)
    seqs = [list(range(NSAMP, NSAMP + NPA))]
    EPn = [("EP", i) for i in range(4)]
    S.op("dve", lambda e: e.memset(FB[:], 0.0), writes=["FB"])
    for sq in seqs:
        for d in range(2):
            if len(sq) > 1:
                order = sq[:NPA - NPO] if d == 0 else sq[::-1][:-1]
            else:
                order = sq
            prev = None
            for k in order:
                o = FB[:, d, k * 8:(k + 1) * 8]
                if prev is None:
                    S.op("dve", lambda e, o=o, d=d, k=k: e.tensor_copy(out=o, in_=EP[:, d, k * 8:(k + 1) * 8]), reads=EPn, writes=["FB"])
                else:
                    pv = FB[:, d, prev * 8:(prev + 1) * 8]
                    tt("dve", o, EP[:, 2 + d, k * 8:(k + 1) * 8], pv, ALU.mult, EPn + ["FB"], ["FB"])
                    tt("dve", o, o, EP[:, d, k * 8:(k + 1) * 8], ALU.add, EPn + ["FB"], ["FB"])
                prev = k
    S.op("dve", lambda e: e.memset(ini[:], 0.0), writes=["ini"])
    for i in range(NSS):
        for r in range(SPS):
            k = i * SPS + r
            if r < SPS - 1:
                S.op("dve", lambda e, k=k: e.tensor_copy(out=ini[:, 1, k * 8:(k + 1) * 8], in_=EP[:, 1, (k + 1) * 8:(k + 2) * 8]), reads=EPn, writes=["ini"])
    for o in range(NPO):
        s = NSAMP + o
        for d in range(2):
            if d == 0 and o > 0:
                continue
            for q in range(NPA):
                kq = NSAMP + q
                mcol = (d * NPO + o) * NPA + q
                stt(ini[:, d, s * 8:(s + 1) * 8], FB[:, d, kq * 8:(kq + 1) * 8], msk[:, mcol:mcol + 1], ini[:, d, s * 8:(s + 1) * 8],
                    ALU.mult, ALU.add, ["FB", "msk", "ini"], ["ini"])
    S.barrier()
    for s in range(NMAIN if cfg.get("stop") in (None, "apath", "b1", "b2", "b3", "b4") else 0):
        xi = s if s < NSAMP else NP1 + (s - NSAMP)
        if s == 0 or cfg.get("stop") is not None:
            load_x(xi)
        nxi = None
        if s + 1 < NMAIN and cfg.get("stop") is None:
            nxi = (s + 1) if (s + 1) < NSAMP else NP1 + (s + 1 - NSAMP)
        apath_seg(True, s)
        if cfg.get("barriers", 0):
            S.barrier()
        for t in range(TT if cfg.get("stop") != "apath" else 0):
            phase_b(s, xi, t, nxi)
        if cfg.get("barriers", 0):
            S.barrier()
    S.finish("sp")
    S.emit()
    return nc


def pieces_lhsT(W, ncols_per_piece=128):
    K, C = W.shape
    a = W.reshape(K // 128, 128, C // 128, 128)
    return np.ascontiguousarray(a.transpose(2, 1, 0, 3).reshape(C // 128, 128, (K // 128) * 128))


def prep_weights(inp):
    VO, NV = vec_layout()
    f = lambda a: np.ascontiguousarray(np.asarray(a, dtype=np.float32))
    w = {}
    pair = lambda a: np.ascontiguousarray(a.reshape(a.shape[0] // 2, 2, 128, a.shape[2]).transpose(0, 2, 1, 3).reshape(a.shape[0] // 2, 128, 2 * a.shape[2]))
    p_in0 = pieces_lhsT(f(inp["even_w_in"][0]))
    w["wA"] = np.ascontiguousarray(np.concatenate([p_in0[g * 8:(g + 1) * 8] for g in range(5)], axis=2))
    w["wO0"] = pair(pieces_lhsT(f(inp["even_w_out"][0])))
    pg = np.concatenate([pieces_lhsT(f(inp["ple_gate_w"][li])) for li in range(2)], 0)
    pl = np.concatenate([pieces_lhsT(f(inp["ple_w"][li])) for li in range(2)], 0)
    w["wPP"] = pair(np.concatenate([pg, pl], axis=2))
    wi = f(inp["odd_w_in"][0])
    w["wUG"] = pair(np.concatenate([pieces_lhsT(wi[:, 0:2048]), pieces_lhsT(wi[:, 4096:6144])], axis=2))
    wv = wi[:, 2048:4096].reshape(2, 4, 128, 4, 512)
    w["wV"] = pair(np.ascontiguousarray(wv.transpose(3, 0, 2, 1, 4).reshape(8, 128, 2048)))
    w["wO1"] = pair(pieces_lhsT(f(inp["odd_w_out"][0])))
    w["sguT"] = np.ascontiguousarray(f(inp["sgu_w"][0]).transpose(2, 0, 1))
    w["gw"] = np.ascontiguousarray(np.stack([f(inp["lru_r_w"][0][0]), f(inp["lru_i_w"][0][0]),
                                             f(inp["lru_r_w"][0][1]), f(inp["lru_i_w"][0][1])], 0))
    vec = np.zeros((128, NV), np.float32)
    pc = lambda v: f(v).reshape(-1, 128).T
    caw = f(inp["conv_a_w"][0])
    vec[:, VO["caw"]:VO["caw"] + 32] = caw.reshape(4, 8, 128).transpose(2, 1, 0).reshape(128, 32)
    vec[:, VO["cab"]:VO["cab"] + 8] = pc(inp["conv_a_b"][0])
    gb = [inp["lru_r_b"][0][0], inp["lru_i_b"][0][0], inp["lru_r_b"][0][1], inp["lru_i_b"][0][1]]
    for i, g in enumerate(gb):
        vec[:, VO["gb"] + i * 8:VO["gb"] + (i + 1) * 8] = pc(f(g).reshape(-1))
    for d in range(2):
        vec[:, VO["lam"] + d * 8:VO["lam"] + (d + 1) * 8] = pc(inp["lru_lam"][0][d])
    cbw = f(inp["conv_b_w"][0])
    vec[:, VO["cbw"]:VO["cbw"] + 248] = cbw.reshape(31, 8, 128).transpose(2, 1, 0).reshape(128, 248)
    vec[:, VO["cbb"]:VO["cbb"] + 8] = pc(inp["conv_b_b"][0])
    vec[:, VO["lnbg"]:VO["lnbg"] + 8] = pc(inp["lnb_g"][0])
    vec[:, VO["lnbb"]:VO["lnbb"] + 8] = pc(inp["lnb_b"][0])
    for li in range(2):
        vec[:, VO["lng"] + li * 8:VO["lng"] + (li + 1) * 8] = pc(inp["ln_g"][li])
        vec[:, VO["lnb"] + li * 8:VO["lnb"] + (li + 1) * 8] = pc(inp["ln_b"][li])
    vec[:, VO["lnvg"]:VO["lnvg"] + 16] = pc(inp["lnv_g"][0])
    w["vec"] = vec
    w["lnvb"] = f(inp["lnv_b"][0]).reshape(1, 2048)
    w["sgub"] = f(inp["sgu_b"][0]).reshape(1, 2048)
    w["ident"] = np.eye(128, dtype=np.float32)
    return w


def ext_segments(xseq, SL):
    Sq, D = xseq.shape
    xp = np.zeros((Sq + 30, D), np.float32)
    xp[15:15 + Sq] = xseq
    n = Sq // SL
    out = np.empty((n, D, SL + 30), np.float32)
    for i in range(n):
        out[i] = xp[i * SL:i * SL + SL + 30].T
    return out


def prep_core(cfg, x_samp, p_samp, x_prm, p_prm, own_start):
    SL = cfg["SL"]; NPA = cfg["n_prompt_all"]; NPO = cfg["n_prompt_own"]
    segs = [ext_segments(x_samp[i], SL) for i in range(x_samp.shape[0])]
    pr = ext_segments(x_prm, SL)
    xe = np.concatenate(segs + [pr, pr[own_start:own_start + NPO]], 0)
    pes = []
    for i in range(x_samp.shape[0]):
        ps_ = p_samp[:, i]
        for r in range(ps_.shape[1] // SL):
            pes.append(ps_[:, r * SL:(r + 1) * SL].transpose(0, 2, 1))
    for o in range(NPO):
        q = own_start + o
        pes.append(p_prm[:, q * SL:(q + 1) * SL].transpose(0, 2, 1))
    pe = np.ascontiguousarray(np.stack(pes, 0), dtype=np.float32)
    m = np.zeros((2, NPO, NPA), np.float32)
    for o in range(NPO):
        q = own_start + o
        if q - 1 >= 0:
            m[0, o, q - 1] = 1.0
        if q + 1 < NPA:
            m[1, o, q + 1] = 1.0
    msk = np.ascontiguousarray(np.broadcast_to(m.reshape(1, -1), (128, 2 * NPO * NPA)))
    return {"xe": np.ascontiguousarray(xe), "pe": pe, "msk": msk}


FULL = {"SL": 1024, "n_samp": 4, "segs_per_samp": 2, "n_prompt_all": 16, "n_prompt_own": 4}


def kernel(**inp):
    cfg = FULL
    SL = cfg["SL"]
    w = prep_weights(inp)
    xs = np.asarray(inp["x_sample"], np.float32); xp = np.asarray(inp["x_prompt"], np.float32)
    pss = np.asarray(inp["p_sample"], np.float32); pp = np.asarray(inp["p_prompt"], np.float32)
    in_maps = []
    for c in range(8):
        d = prep_core(cfg, xs[c * 4:(c + 1) * 4], pss[:, c * 4:(c + 1) * 4], xp[c // 4], pp[:, c // 4], (c % 4) * cfg["n_prompt_own"])
        d.update(w)
        in_maps.append(d)
    nc = build(cfg)
    res = run_bass_kernel_spmd(nc, in_maps, core_ids=list(range(8)))
    y_s = np.empty_like(xs); y_p = np.empty_like(xp)
    SPS = cfg["segs_per_samp"]; NPO = cfg["n_prompt_own"]
    for c in range(8):
        y = res.results[c]["y"]
        for i in range(4):
            for r in range(SPS):
                y_s[c * 4 + i, r * SL:(r + 1) * SL] = y[i * SPS + r].T
        for o in range(NPO):
            q = (c % 4) * NPO + o
            y_p[c // 4, q * SL:(q + 1) * SL] = y[4 * SPS + o].T
    return (y_p, y_s)
```
